# Optimizing a Trainium2 kernel written in Bass

```python
import jax, jax.numpy as jnp

D_MODEL = 2048
BATCH = 4
SEQ = 4096
DEPTH = 1

CTX_LEN = 256
GRID_W = 64

HEAD_DIM = 128
N_HEADS = 8
N_KV_HEADS = 2
ATTN_WIDTH = N_HEADS * HEAD_DIM
KV_WIDTH = N_KV_HEADS * HEAD_DIM
WINDOW = 128
ATTN_BLOCK = 128
ROPE_BASE = 10000.0

CHUNK = 128
N_GMLP_GROUPS = 8
GMLP_WIDTH = 1024
GMLP_GROUP_DIM = GMLP_WIDTH // N_GMLP_GROUPS

D_FF = 5632
FFN_RES_WEIGHT = 0.5
N_MOD = 9
NORM_EPS = 1e-6
NEG_INF = -1e30

PROJ_WIDTH = ATTN_WIDTH + 2 * KV_WIDTH + 2 * GMLP_WIDTH + 2 * D_MODEL

kernel_name = "hybrid_dit_gmlp_window_gqa_macaron"


def _rmsnorm(t, g):
    tf = t.astype(jnp.float32)
    y = tf * jax.lax.rsqrt(jnp.mean(tf * tf, axis=-1, keepdims=True) + NORM_EPS)
    return (y * g.astype(jnp.float32)).astype(t.dtype)


def _layernorm(t, g, b):
    tf = t.astype(jnp.float32)
    mu = jnp.mean(tf, axis=-1, keepdims=True)
    var = jnp.mean(jnp.square(tf - mu), axis=-1, keepdims=True)
    y = (tf - mu) * jax.lax.rsqrt(var + NORM_EPS)
    return (y * g.astype(jnp.float32) + b.astype(jnp.float32)).astype(t.dtype)


def _modulate(t, g_pre, shift, scale):
    return _rmsnorm(t, g_pre) * (1.0 + scale) + shift


def _swiglu(h, w_in, w_out):
    gt, up = jnp.split(h @ w_in, 2, axis=-1)
    return (jax.nn.silu(gt) * up) @ w_out


def _ffn_step(t, mod, j, g, w_in, w_out):
    h = _modulate(t, g[2 * j], mod[:, 3 * j], mod[:, 3 * j + 1])
    y = _rmsnorm(_swiglu(h, w_in, w_out), g[2 * j + 1])
    return t + FFN_RES_WEIGHT * mod[:, 3 * j + 2] * y


def _axial_rope_tables(rows):
    row = jnp.repeat(jnp.arange(rows), GRID_W).astype(jnp.float32)
    col = jnp.tile(jnp.arange(GRID_W), rows).astype(jnp.float32)
    quarter = HEAD_DIM // 4
    freqs = ROPE_BASE ** (-jnp.arange(quarter, dtype=jnp.float32) / quarter)
    ang_r = row[:, None] * freqs[None, :]
    ang_c = col[:, None] * freqs[None, :]
    ang = jnp.concatenate([ang_r, ang_r, ang_c, ang_c], axis=-1)
    return jnp.cos(ang)[:, None, :], jnp.sin(ang)[:, None, :]


def _rope(t, cos, sin):
    ts = t.reshape(t.shape[:-1] + (2, 2, HEAD_DIM // 4))
    rot = jnp.concatenate([-ts[..., 1:, :], ts[..., :1, :]], axis=-2).reshape(t.shape)
    return (t.astype(jnp.float32) * cos + rot.astype(jnp.float32) * sin).astype(t.dtype)


def _chunk_mlp(u, v, ln_g, ln_b, w_s, b_s):
    bsz, n, _ = u.shape
    v = _layernorm(v, ln_g, ln_b)
    vc = v.reshape(bsz, n // CHUNK, CHUNK, N_GMLP_GROUPS, GMLP_GROUP_DIM)
    mixed = jnp.einsum('gpq,bnqgd->bnpgd', w_s, vc) + b_s.T[None, None, :, :, None]
    return u * mixed.reshape(bsz, n, GMLP_WIDTH)


def _window_attention(q, k, v, k_ctx, v_ctx, sink):
    bsz, n = q.shape[0], q.shape[1]
    nb = n // ATTN_BLOCK
    rep = N_HEADS // N_KV_HEADS
    c_len = k_ctx.shape[1]
    qb = q.reshape(bsz, nb, ATTN_BLOCK, N_KV_HEADS, rep, HEAD_DIM)

    def band(t):
        tp = jnp.pad(t, ((0, 0), (ATTN_BLOCK, ATTN_BLOCK), (0, 0), (0, 0)))
        tp = tp.reshape(bsz, nb + 2, ATTN_BLOCK, N_KV_HEADS, HEAD_DIM)
        return jnp.concatenate([tp[:, :-2], tp[:, 1:-1], tp[:, 2:]], axis=2)

    kb, vb = band(k), band(v)
    scale = HEAD_DIM ** -0.5
    s_loc = jnp.einsum('bnqgrd,bnkgd->bngrqk', qb, kb).astype(jnp.float32) * scale
    blk = jnp.arange(nb)[:, None, None]
    qpos = blk * ATTN_BLOCK + jnp.arange(ATTN_BLOCK)[None, :, None]
    kpos = (blk - 1) * ATTN_BLOCK + jnp.arange(3 * ATTN_BLOCK)[None, None, :]
    valid = (jnp.abs(qpos - kpos) <= WINDOW) & (kpos >= 0) & (kpos < n)
    s_loc = jnp.where(valid[None, :, None, None], s_loc, NEG_INF)
    s_ctx = jnp.einsum('bnqgrd,bcgd->bngrqc', qb, k_ctx).astype(jnp.float32) * scale
    s_sink = jnp.broadcast_to(sink.astype(jnp.float32).reshape(1, 1, N_KV_HEADS, rep, 1, 1),
                              s_loc.shape[:-1] + (1,))
    p = jax.nn.softmax(jnp.concatenate([s_loc, s_ctx, s_sink], axis=-1), axis=-1)
    p_loc = p[..., :3 * ATTN_BLOCK].astype(v.dtype)
    p_ctx = p[..., 3 * ATTN_BLOCK:3 * ATTN_BLOCK + c_len].astype(v.dtype)
    out = (jnp.einsum('bngrqk,bnkgd->bnqgrd', p_loc, vb)
           + jnp.einsum('bngrqc,bcgd->bnqgrd', p_ctx, v_ctx))
    return out.reshape(bsz, n, ATTN_WIDTH)


def _context_attention(q, k, v, sink):
    bsz, c_len = q.shape[0], q.shape[1]
    rep = N_HEADS // N_KV_HEADS
    qg = q.reshape(bsz, c_len, N_KV_HEADS, rep, HEAD_DIM)
    s = jnp.einsum('bqgrd,bkgd->bgrqk', qg, k).astype(jnp.float32) * (HEAD_DIM ** -0.5)
    s_sink = jnp.broadcast_to(sink.astype(jnp.float32).reshape(1, N_KV_HEADS, rep, 1, 1),
                              s.shape[:-1] + (1,))
    p = jax.nn.softmax(jnp.concatenate([s, s_sink], axis=-1), axis=-1)[..., :c_len].astype(v.dtype)
    return jnp.einsum('bgrqk,bkgd->bqgrd', p, v).reshape(bsz, c_len, ATTN_WIDTH)


def _project(t, w_in):
    bsz, n = t.shape[0], t.shape[1]
    p = t @ w_in
    o1 = ATTN_WIDTH
    o2 = o1 + KV_WIDTH
    o3 = o2 + KV_WIDTH
    o4 = o3 + GMLP_WIDTH
    o5 = o4 + GMLP_WIDTH
    o6 = o5 + D_MODEL
    q, k, v, u, vg, ga, gb = jnp.split(p, [o1, o2, o3, o4, o5, o6], axis=-1)
    q = q.reshape(bsz, n, N_HEADS, HEAD_DIM)
    k = k.reshape(bsz, n, N_KV_HEADS, HEAD_DIM)
    v = v.reshape(bsz, n, N_KV_HEADS, HEAD_DIM)
    return q, k, v, jax.nn.gelu(u), jax.nn.gelu(vg), ga, gb


def _merge(mix_a, mix_b, ga, gb, w_br_a, w_br_b, w_out):
    merged = jax.nn.sigmoid(ga) * (mix_a @ w_br_a) + jax.nn.sigmoid(gb) * (mix_b @ w_br_b)
    return merged @ w_out


def _mixer(h, h_ctx, w_in, ln_g, ln_b, w_s, b_s, sink, w_br_a, w_br_b, w_out, cos, sin, with_ctx_out):
    q_c, k_c, v_c, u_c, vg_c, ga_c, gb_c = _project(h_ctx, w_in)
    q, k, v, u, vg, ga, gb = _project(h, w_in)
    q, k = _rope(q, cos, sin), _rope(k, cos, sin)
    mix_a = _chunk_mlp(u, vg, ln_g, ln_b, w_s, b_s)
    mix_b = _window_attention(q, k, v, k_c, v_c, sink)
    y = _merge(mix_a, mix_b, ga, gb, w_br_a, w_br_b, w_out)
    y_ctx = None
    if with_ctx_out:
        mix_a_c = _chunk_mlp(u_c, vg_c, ln_g, ln_b, w_s, b_s)
        mix_b_c = _context_attention(q_c, k_c, v_c, sink)
        y_ctx = _merge(mix_a_c, mix_b_c, ga_c, gb_c, w_br_a, w_br_b, w_out)
    return y, y_ctx


def setup_inputs(seed: int = 0) -> dict:
    key = jax.random.key(seed)
    ks = jax.random.split(key, 20)
    f32 = jnp.float32
    L, D, F = DEPTH, D_MODEL, D_FF
    nrm = lambda k, shape, s: jax.random.normal(k, shape, f32) * s
    return {
        "x": nrm(ks[0], (BATCH, SEQ, D), 1.0),
        "c": nrm(ks[1], (BATCH, D), 1.0),
        "ctx": nrm(ks[2], (BATCH, CTX_LEN, D), 1.0),
        "c_ctx": nrm(ks[3], (D,), 1.0),
        "w_ada": nrm(ks[4], (L, D, N_MOD * D), 0.5 * D ** -0.5),
        "b_ada": nrm(ks[5], (L, N_MOD * D), 0.02),
        "norm_g": 1.0 + nrm(ks[6], (L, 6, D), 0.1),
        "w_ffn_in": nrm(ks[7], (L, 2, D, 2 * F), D ** -0.5),
        "w_ffn_out": nrm(ks[8], (L, 2, F, D), F ** -0.5),
        "w_in": nrm(ks[9], (L, D, PROJ_WIDTH), D ** -0.5),
        "gmlp_ln_g": 1.0 + nrm(ks[10], (L, GMLP_WIDTH), 0.1),
        "gmlp_ln_b": nrm(ks[11], (L, GMLP_WIDTH), 0.02),
        "w_spatial": nrm(ks[12], (L, N_GMLP_GROUPS, CHUNK, CHUNK), CHUNK ** -0.5),
        "b_spatial": 1.0 + nrm(ks[13], (L, N_GMLP_GROUPS, CHUNK), 0.1),
        "attn_sink": nrm(ks[14], (L, N_HEADS), 1.0),
        "w_branch_a": nrm(ks[15], (L, GMLP_WIDTH, D), GMLP_WIDTH ** -0.5),
        "w_branch_b": nrm(ks[16], (L, ATTN_WIDTH, D), ATTN_WIDTH ** -0.5),
        "w_out": nrm(ks[17], (L, D, D), D ** -0.5),
    }


def reference(x, c, ctx, c_ctx, w_ada, b_ada, norm_g, w_ffn_in, w_ffn_out, w_in,
              gmlp_ln_g, gmlp_ln_b, w_spatial, b_spatial, attn_sink,
              w_branch_a, w_branch_b, w_out):
    bsz, n = x.shape[0], x.shape[1]
    rows = n // GRID_W
    cos, sin = _axial_rope_tables(rows)
    cos, sin = cos.astype(jnp.float32), sin.astype(jnp.float32)
    for l in range(DEPTH):
        last = l == DEPTH - 1
        g = norm_g[l]
        mod = (jax.nn.silu(c) @ w_ada[l] + b_ada[l]).reshape(bsz, N_MOD, 1, D_MODEL)
        mod_c = (jax.nn.silu(c_ctx) @ w_ada[l] + b_ada[l]).reshape(1, N_MOD, 1, D_MODEL)

        x = _ffn_step(x, mod, 0, g, w_ffn_in[l, 0], w_ffn_out[l, 0])
        ctx = _ffn_step(ctx, mod_c, 0, g, w_ffn_in[l, 0], w_ffn_out[l, 0])

        h = _modulate(x, g[2], mod[:, 3], mod[:, 4])
        h_ctx = _modulate(ctx, g[2], mod_c[:, 3], mod_c[:, 4])
        y, y_ctx = _mixer(h, h_ctx, w_in[l], gmlp_ln_g[l], gmlp_ln_b[l], w_spatial[l],
                          b_spatial[l], attn_sink[l], w_branch_a[l], w_branch_b[l],
                          w_out[l], cos, sin, not last)
        x = x + mod[:, 5] * _rmsnorm(y, g[3])

        x = _ffn_step(x, mod, 2, g, w_ffn_in[l, 1], w_ffn_out[l, 1])
        if not last:
            ctx = ctx + mod_c[:, 5] * _rmsnorm(y_ctx, g[3])
            ctx = _ffn_step(ctx, mod_c, 2, g, w_ffn_in[l, 1], w_ffn_out[l, 1])
    return x
```

```python
import contextlib
import numpy as np
import concourse.bass as bass
import concourse.mybir as mybir
from concourse.bass_utils import run_bass_kernel_spmd

F32 = mybir.dt.float32
BF16 = mybir.dt.bfloat16
AF = mybir.ActivationFunctionType
ALU = mybir.AluOpType
AX = mybir.AxisListType

D = 2048
DFF = 5632
NOWN = 16
NBLK = 19
HALO = 16
CTX0 = 17
EPS = 1e-6
SCALE = 128.0 ** -0.5
NSLOT = 6
STOP = None
DEBUG = False


class _Stop(Exception):
    pass


def _ck(name):
    if STOP == name:
        raise _Stop()


class Buf:
    __slots__ = ("name", "w", "r", "prev_r", "dsem", "dval")

    def __init__(self, name):
        self.name = name
        self.w = {}
        self.r = {}
        self.prev_r = {}
        self.dsem = None
        self.dval = 0


def _merge(dst, src):
    for k, v in src.items():
        if dst.get(k, 0) < v:
            dst[k] = v


class Stream:
    def __init__(self, name, sem_id):
        self.name = name
        self.sem_id = sem_id
        self.count = 0
        self.ops = []
        self.seen = {}


class Sched:
    ENGS = ("pe", "act", "dve", "pool", "sp")

    def __init__(self, nc, stack):
        self.nc = nc
        self.stack = stack
        self.sems = []
        self.streams = {}
        for n in self.ENGS:
            self.streams[n] = Stream(n, self.new_sem("c_" + n))
        self.nbuf = 0

    def new_sem(self, name):
        h = self.stack.enter_context(self.nc.semaphore(name))
        self.sems.append(h)
        return len(self.sems) - 1

    def buf(self, name=None, dma=False):
        self.nbuf += 1
        b = Buf(name or f"b{self.nbuf}")
        if dma:
            b.dsem = self.new_sem("d%d" % self.nbuf)
        return b

    def op(self, eng, fn, reads=(), writes=(), pwrites=(), dma=None):
        st = self.streams[eng]
        raw = {}
        other = {}
        for b in reads:
            _merge(raw, b.w)
        for b in writes:
            _merge(other, b.r)
            _merge(other, b.w)
        for b in pwrites:
            _merge(other, b.prev_r)
        _merge(raw, other)
        waits = []
        for k, v in raw.items():
            if st.seen.get(k, 0) < v:
                st.seen[k] = v
                waits.append((k, v))
        if dma is not None:
            dma.dval += 16
            tok = (dma.dsem, dma.dval)
            inc = (dma.dsem, 16)
        else:
            st.count += 1
            tok = (st.sem_id, st.count)
            inc = (st.sem_id, 1)
        st.ops.append((waits, fn, inc))
        for b in reads:
            if b.r.get(tok[0], 0) < tok[1]:
                b.r[tok[0]] = tok[1]
        for b in writes:
            b.prev_r = b.r
            b.r = {}
            b.w = {tok[0]: tok[1]}
        for b in pwrites:
            if b.w.get(tok[0], 0) < tok[1]:
                b.w[tok[0]] = tok[1]
        return tok

    def wait_all(self, eng, bufs):
        st = self.streams[eng]
        deps = {}
        for b in bufs:
            _merge(deps, b.r)
            _merge(deps, b.w)
        waits = []
        for k, v in deps.items():
            if st.seen.get(k, 0) < v:
                st.seen[k] = v
                waits.append((k, v))
        st.ops.append((waits, None, None))

    def barrier(self):
        cur = {self.streams[n].sem_id: self.streams[n].count for n in ("pe", "act", "dve", "pool")}
        for n in self.ENGS:
            st = self.streams[n]
            waits = []
            for k, v in cur.items():
                if v > 0 and st.seen.get(k, 0) < v:
                    st.seen[k] = v
                    waits.append((k, v))
            st.ops.append((waits, None, None))

    def emit(self):
        nc = self.nc
        sems = self.sems

        def replay(st):
            def run(e):
                for waits, fn, inc in st.ops:
                    for k, v in waits:
                        e.wait_ge(sems[k], v)
                    if fn is not None:
                        ins = fn(e)
                        ins.then_inc(sems[inc[0]], inc[1])
            return run

        with nc.Block() as block:
            block.tensor(replay(self.streams["pe"]))
            block.scalar(replay(self.streams["act"]))
            block.vector(replay(self.streams["dve"]))
            block.gpsimd(replay(self.streams["pool"]))
            block.sync(replay(self.streams["sp"]))


def build_program():
    nc = bass.Bass("TRN2", target_bir_lowering=False)

    def din(name, shape):
        return nc.dram_tensor(name, shape, F32, kind="ExternalInput").ap()

    xe = din("xe", [NBLK * 128, D])
    cvec = din("cvec", [128, 32])
    w_ada = din("w_ada", [D, 9 * D])
    bpp_d = din("bpp", [128, 144])
    gpp_d = din("gpp", [128, 96])
    w_ffn_in = din("w_ffn_in", [2, D, 2 * DFF])
    w_ffn_out = din("w_ffn_out", [2, DFF, D])
    w_in = din("w_in", [D, 7680])
    lngb = din("lngb", [128, 16])
    w_sT = din("w_sT", [128, 8, 128])
    b_sp = din("b_sp", [1, 1024])
    sink = din("sink", [1, 8])
    w_bra = din("w_bra", [1024, D])
    w_brb = din("w_brb", [1024, D])
    w_o = din("w_o", [D, D])
    cosT = din("cosT", [128, 17 * 128])
    sinT = din("sinT", [128, 17 * 128])
    rmatT = din("rmatT", [128, 128])
    masks = din("masks", [128, 4, 128])
    x1 = nc.dram_tensor("x1", [NBLK * 128, D], F32).ap()
    x2 = nc.dram_tensor("x2", [NOWN * 128, D], F32).ap()
    x1d = x2d = None
    if DEBUG:
        x1d = nc.dram_tensor("x1d", [NBLK * 128, D], F32, kind="ExternalOutput").ap()
        x2d = nc.dram_tensor("x2d", [NOWN * 128, D], F32, kind="ExternalOutput").ap()
    dbgB = []
    dd = {}
    if DEBUG:
        for nm, shp in (("d_q", [128, 8, 512]), ("d_k", [128, 2, NBLK * 128]), ("d_v", [128, NBLK, 256]),
                        ("d_ma", [128, 8, 512]), ("d_mb", [128, 8, 512]), ("d_mg", [128, 16, 512]), ("d_h", [128, 16, 512])):
            dd[nm] = nc.dram_tensor(nm, shp, BF16, kind="ExternalOutput").ap()
    outp = nc.dram_tensor("out", [NOWN * 128, D], F32, kind="ExternalOutput").ap()

    with contextlib.ExitStack() as stk:
        S = Sched(nc, stk)

        def sb(name, shape, dt):
            return stk.enter_context(nc.sbuf_tensor(name, shape, dt))

        PS = [stk.enter_context(nc.psum_tensor(f"ps{i}", [128, 512], F32)) for i in range(8)]
        PB = [S.buf(f"ps{i}") for i in range(8)]

        slots = [sb(f"wsl{i}", [128, 16, 256], BF16) for i in range(NSLOT)]
        slotB = [S.buf(f"wsl{i}", dma=True) for i in range(NSLOT)]
        slot_ctr = [0]

        def next_slot():
            i = slot_ctr[0] % NSLOT
            slot_ctr[0] += 1
            return slots[i], slotB[i]

        Gt = [sb("G0", [128, D], F32)]
        GB = [S.buf("G0", dma=True)]
        xs = [sb(f"xs{i}", [128, D], F32) for i in range(2)]
        xsB = [S.buf(f"xs{i}", dma=True) for i in range(2)]
        xs_ctr = [0]

        def next_xs():
            i = xs_ctr[0] % 2
            xs_ctr[0] += 1
            return xs[i], xsB[i]

        xns = [sb(f"xn{i}", [128, D], BF16) for i in range(2)]
        xnsB = [S.buf(f"xn{i}") for i in range(2)]
        xn_ctr = [0]

        def next_xn():
            i = xn_ctr[0] % 2
            xn_ctr[0] += 1
            return xns[i], xnsB[i]

        hT = sb("hT", [128, 16, 512], BF16)
        hTB = [S.buf(f"hT{i}") for i in range(4)]
        ystage = sb("ystage", [128, 16, 512], F32)
        ysB = [S.buf(f"ys{i}") for i in range(16)]
        arena = sb("arena", [128, 26 * 1024], BF16)
        h1 = arena[:, 0:44 * 512].rearrange("p (f t) -> p f t", f=44)
        h1B = [S.buf(f"h1_{i}") for i in range(44)]
        def aview(off_kb, nchunk, t):
            o = off_kb * 512
            return arena[:, o:o + nchunk * t].rearrange("p (c t) -> p c t", c=nchunk)
        qT = aview(0, 8, 512)
        mbT = qT
        uT = aview(8, 8, 512)
        vn_tm = aview(16, 4, 1024)
        vnT = aview(24, 8, 512)
        mgT = aview(16, 16, 512)
        kT_all = arena[:, 32 * 512:32 * 512 + 2 * NBLK * 128].rearrange("p (g t) -> p g t", g=2)
        v_all = arena[:, 32 * 512 + 2 * NBLK * 128:32 * 512 + 4 * NBLK * 128].rearrange("p (b c) -> p b c", b=NBLK)
        qTB = [S.buf(f"qT{i}") for i in range(8)]
        uTB = [S.buf(f"uT{i}") for i in range(8)]
        vnTB = [S.buf(f"vnT{i}") for i in range(8)]
        vntmB = [S.buf(f"vntm{i}") for i in range(4)]
        mbTB = [[S.buf(f"mbT{b}_{g}") for g in range(2)] for b in range(4)]
        mgTB = [S.buf(f"mgT{i}") for i in range(16)]
        kTB = [S.buf(f"kT{i}") for i in range(NBLK)]
        vB = [S.buf(f"v{i}") for i in range(NBLK)]
        scx = arena[:, 0:2048].rearrange("p (k m) -> p k m", k=16)
        scc = arena[:, 2048:4096].rearrange("p (k m) -> p k m", k=16)
        bada_bf = arena[0:1, 4096:4096 + 2048]
        scB = S.buf("sc")
        badaB = S.buf("bada", dma=True)

        T = [sb(f"T{i}", [128, 512], F32) for i in range(4)]
        TB = [S.buf(f"T{i}") for i in range(4)]
        t_ctr = [0]

        def next_T():
            i = t_ctr[0] % 4
            t_ctr[0] += 1
            return T[i], TB[i]

        Tb = [sb(f"Tb{i}", [128, 512], BF16) for i in range(4)]
        TbB = [S.buf(f"Tb{i}") for i in range(4)]
        tb_ctr = [0]

        def next_Tb():
            i = tb_ctr[0] % 4
            tb_ctr[0] += 1
            return Tb[i], TbB[i]

        ident_f = sb("ident_f", [128, 128], F32)
        ident_bf = sb("ident_bf", [128, 128], BF16)
        ones_f = sb("ones_f", [128, 128], F32)
        ones_bf = sb("ones_bf", [128, 128], BF16)
        rmat = sb("rmat", [128, 128], BF16)
        mask_sb = sb("mask_sb", [128, 4, 128], BF16)
        wsT_sb = sb("wsT_sb", [128, 8, 128], BF16)
        bs_sb = sb("bs_sb", [1, 1024], BF16)
        es_hi = sb("es_hi", [1, 8], BF16)
        es_lo = sb("es_lo", [1, 8], BF16)
        sink_sb = sb("sink_sb", [1, 8], F32)
        lngb_sb = sb("lngb_sb", [128, 16], F32)
        cvec_sb = sb("cvec_sb", [128, 32], F32)
        sc_sb = sb("sc_sb", [128, 32], F32)
        sc_bf = sb("sc_bf", [128, 32], BF16)
        bpp = sb("bpp_sb", [128, 144], F32)
        gpp = sb("gpp_sb", [128, 96], F32)
        one11 = sb("one11", [1, 1], F32)
        NV = 14
        vecs = sb("vecs", [128, NV, 16], F32)
        vecB = [S.buf(f"vec{i}") for i in range(NV)]
        VIDX = {n: i for i, n in enumerate(["B0x", "A0x", "G0x", "B1x", "A1x", "G1x", "B2x", "A2x", "G2x",
                                            "B0c", "A0c", "G0c", "B1c", "A1c"])}
        cos_sb = sb("cos_sb", [128, 512], F32)
        sin_sb = sb("sin_sb", [128, 512], F32)
        csB = S.buf("cossin", dma=True)
        small = sb("small", [128, 32], F32)
        small2 = sb("small2", [128, 8], F32)
        small3 = sb("small3", [128, 8], F32)
        sm3B = [S.buf(f"sm3_{i}") for i in range(4)]
        smallB = [S.buf(f"small{i}") for i in range(4)]
        epsT = sb("epsT", [128, 1], F32)
        constB = S.buf("consts", dma=True)

        def act(out, in_, func, reads, writes, **kw):
            S.op("act", lambda e: e.activation(out=out, in_=in_, func=func, **kw), reads=reads, writes=writes)

        def dve_tt(out, in0, in1, op, reads, writes, eng="dve"):
            S.op(eng, lambda e: e.tensor_tensor(out=out, in0=in0, in1=in1, op=op), reads=reads, writes=writes)

        def dve_stt(out, in0, scalar, in1, op0, op1, reads, writes):
            S.op("dve", lambda e: e.scalar_tensor_tensor(out=out, in0=in0, scalar=scalar, in1=in1, op0=op0, op1=op1),
                 reads=reads, writes=writes)

        def dve_ts(out, in0, s1, s2, op0, op1, reads, writes):
            if op1 is None:
                S.op("dve", lambda e: e.tensor_scalar(out=out, in0=in0, scalar1=s1, scalar2=None, op0=op0),
                     reads=reads, writes=writes)
            else:
                S.op("dve", lambda e: e.tensor_scalar(out=out, in0=in0, scalar1=s1, scalar2=s2, op0=op0, op1=op1),
                     reads=reads, writes=writes)

        def dve_copy(out, in_, reads, writes):
            S.op("dve", lambda e: e.tensor_copy(out=out, in_=in_), reads=reads, writes=writes)

        def dve_recip(out, in_, reads, writes):
            S.op("dve", lambda e: e.reciprocal(out=out, in_=in_), reads=reads, writes=writes)

        def mm(out, pairs, reads, writes=(), pwrites=(), start=True, stop=True):
            pairs = list(pairs)

            def fn(e):
                n = len(pairs)
                ins = None
                for i, (l, r) in enumerate(pairs):
                    ins = e.matmul(out, lhsT=l, rhs=r, start=(start and i == 0), stop=(stop and i == n - 1))
                return ins
            S.op("pe", fn, reads=reads, writes=writes, pwrites=pwrites)

        def transposes(items, ident, reads, writes=(), pwrites=()):
            items = list(items)

            def fn(e):
                ins = None
                for o, i in items:
                    ins = e.transpose(o, i, ident)
                return ins
            S.op("pe", fn, reads=reads, writes=writes, pwrites=pwrites)

        def dma(q, out, in_, reads, writes, dbuf):
            S.op(q, lambda e: e.dma_start(out=out, in_=in_), reads=reads, writes=writes, dma=dbuf)

        def load_w(src_ap, nk=16):
            sl, slB = next_slot()
            dma("pool", sl[:, 0:nk, :], src_ap, [], [slB], slB)
            return sl, slB

        def dump(nm, src, rbufs):
            if not DEBUG:
                return
            if not dbgB:
                dbgB.append(S.buf("dbg", dma=True))
            db = dbgB[0]
            S.op("sp", lambda e: e.dma_start(out=dd[nm], in_=src), reads=rbufs, pwrites=[db], dma=db)

        S.op("pool", lambda e: e.memset(ident_f[:], 0.0), writes=[constB])
        S.op("pool", lambda e: e.affine_select(out=ident_f[:], in_=ident_f[:], pattern=[[-1, 128]],
                                               compare_op=ALU.not_equal, fill=1.0, base=0, channel_multiplier=1),
             reads=[constB], writes=[constB])
        S.op("dve", lambda e: e.tensor_copy(out=ident_bf[:], in_=ident_f[:]), reads=[constB], pwrites=[constB])
        S.op("dve", lambda e: e.memset(ones_f[:], 1.0), pwrites=[constB])
        S.op("dve", lambda e: e.memset(ones_bf[:], 1.0), pwrites=[constB])
        S.op("dve", lambda e: e.memset(epsT[:], EPS), pwrites=[constB])
        cB2 = S.buf("consts2", dma=True)
        for (dst, src) in ((rmat[:], rmatT), (mask_sb[:], masks), (wsT_sb[:], w_sT), (bs_sb[:], b_sp)):
            S.op("pool", (lambda e, dst=dst, src=src: e.dma_start(out=dst, in_=src)), pwrites=[cB2], dma=cB2)
        cB3 = S.buf("consts3", dma=True)
        S.op("dve", lambda e: e.memset(one11[:], 1.0), pwrites=[constB])
        for (dst, src) in ((sink_sb[:], sink), (lngb_sb[:], lngb), (cvec_sb[:], cvec), (bpp[:], bpp_d), (gpp[:], gpp_d)):
            S.op("sp", (lambda e, dst=dst, src=src: e.dma_start(out=dst, in_=src)), pwrites=[cB3], dma=cB3)
        esB = S.buf("es")
        act(sink_sb[:], sink_sb[:], AF.Exp, [cB3], [esB])
        dve_copy(es_hi[:], sink_sb[:], [esB], [esB])
        dve_tt(sink_sb[:], sink_sb[:], es_hi[:], ALU.subtract, [esB], [esB])
        dve_copy(es_lo[:], sink_sb[:], [esB], [esB])
        CONSTS = [constB, cB2, cB3, esB]

        x1B = [S.buf(f"x1_{i}", dma=True) for i in range(NBLK)]
        x2B = [S.buf(f"x2_{i}", dma=True) for i in range(NOWN)]
        outB = [S.buf(f"out_{i}", dma=True) for i in range(NOWN)]
        try:
            act(sc_sb[:], cvec_sb[:], AF.Silu, [cB3], [scB])
            dve_copy(sc_bf[:], sc_sb[:], [scB], [scB])
            w_ada_v = w_ada.rearrange("(kc p) n -> p kc n", p=128)
            vec_started = set()

            def mod_quarter(j, q4, streams):
                kind = j % 3
                sub = j // 3
                col0 = j * D + q4 * 512
                s0, s0B = load_w(w_ada_v[:, :, col0:col0 + 256])
                s1, s1B = load_w(w_ada_v[:, :, col0 + 256:col0 + 512])
                c0 = q4 * 4
                for si, (sn, off) in enumerate(streams):
                    bank_r = 2 + 2 * si
                    bank_c = 3 + 2 * si
                    mm(PS[bank_r][0:1, 0:256], [(sc_bf[:, off + kc:off + kc + 1], s0[:, kc, :]) for kc in range(16)],
                       [scB, s0B], writes=[PB[bank_r]])
                    mm(PS[bank_r][0:1, 256:512], [(sc_bf[:, off + kc:off + kc + 1], s1[:, kc, :]) for kc in range(16)],
                       [scB, s1B], pwrites=[PB[bank_r]])
                    row, rowB = next_T()
                    act(row[0:1, :], PS[bank_r][0:1, :], AF.Copy, [PB[bank_r]], [rowB])
                    for q in range(4):
                        mm(PS[bank_c][:, q:q + 1], [(row[0:1, q * 128:(q + 1) * 128], one11[0:1, 0:1])], [rowB, constB],
                           writes=[PB[bank_c]] if q == 0 else (), pwrites=() if q == 0 else [PB[bank_c]])
                    nm = ("B", "A", "G")[kind] + str(sub) + sn
                    vi = VIDX[nm]
                    first = nm not in vec_started
                    vec_started.add(nm)
                    wr = [vecB[vi]] if first else []
                    pw = [] if first else [vecB[vi]]
                    dst = vecs[:, vi, c0:c0 + 4]
                    bq_ = bpp[:, j * 16 + c0:j * 16 + c0 + 4]
                    pc = PS[bank_c][:, 0:4]
                    if kind == 0:
                        S.op("dve", (lambda e, dst=dst, pc=pc, bq_=bq_: e.tensor_tensor(out=dst, in0=pc, in1=bq_, op=ALU.add)),
                             reads=[PB[bank_c], cB3], writes=wr, pwrites=pw)
                    else:
                        gq_ = gpp[:, (2 * sub + (kind - 1)) * 16 + c0:(2 * sub + (kind - 1)) * 16 + c0 + 4]
                        tq, tqB = next_T()
                        if kind == 1:
                            S.op("dve", (lambda e, tq=tq, pc=pc, bq_=bq_: e.scalar_tensor_tensor(
                                out=tq[:, 0:4], in0=pc, scalar=1.0, in1=bq_, op0=ALU.add, op1=ALU.add)),
                                reads=[PB[bank_c], cB3], writes=[tqB])
                            S.op("dve", (lambda e, dst=dst, tq=tq, gq_=gq_: e.tensor_tensor(out=dst, in0=tq[:, 0:4], in1=gq_, op=ALU.mult)),
                                 reads=[tqB, cB3], writes=wr, pwrites=pw)
                        else:
                            coef = 1.0 if sub == 1 else 0.5
                            S.op("dve", (lambda e, tq=tq, pc=pc, bq_=bq_: e.tensor_tensor(out=tq[:, 0:4], in0=pc, in1=bq_, op=ALU.add)),
                                 reads=[PB[bank_c], cB3], writes=[tqB])
                            S.op("dve", (lambda e, dst=dst, tq=tq, gq_=gq_, coef=coef: e.scalar_tensor_tensor(
                                out=dst, in0=tq[:, 0:4], scalar=coef, in1=gq_, op0=ALU.mult, op1=ALU.mult)),
                                reads=[tqB, cB3], writes=wr, pwrites=pw)

            for j in range(2):
                for q4 in range(4):
                    mod_quarter(j, q4, [("x", 0), ("c", 16)])
            ffn_extra = [(lambda j=j, q4=q4: mod_quarter(j, q4, [("x", 0), ("c", 16)])) for j in range(2, 5) for q4 in range(4)]
            deferred_mod = [(lambda j=j, q4=q4: mod_quarter(j, q4, [("x", 0)])) for j in range(5, 9) for q4 in range(4)]

            _ck('mod')
            pass

            def expand_vec(vname, gi=0):
                vi = VIDX[vname]
                for q4 in range(4):
                    bank = 4 + q4
                    for c4 in range(4):
                        c = q4 * 4 + c4
                        tt, ttB = next_T()
                        dve_ts(tt[:, 0:128], ident_f[:], vecs[:, vi, c:c + 1], None, ALU.mult, None, [constB, vecB[vi]], [ttB])
                        if c4 == 0:
                            mm(PS[bank][:, c4 * 128:(c4 + 1) * 128], [(ones_f[:], tt[:, 0:128])], [ttB, constB], writes=[PB[bank]])
                        else:
                            mm(PS[bank][:, c4 * 128:(c4 + 1) * 128], [(ones_f[:], tt[:, 0:128])], [ttB, constB], pwrites=[PB[bank]])
                    cols = slice(q4 * 512, (q4 + 1) * 512)
                    if q4 == 0:
                        act(Gt[gi][:, cols], PS[bank][:], AF.Copy, [PB[bank]], [GB[gi]])
                    else:
                        S.op("act", (lambda e, cols=cols, bank=bank: e.activation(out=Gt[gi][:, cols], in_=PS[bank][:], func=AF.Copy)),
                             reads=[PB[bank]], pwrites=[GB[gi]])

            pre_state = {}

            def pre_block(src_ap, srcB, bi, sfx, sub):
                pre_a(src_ap, srcB, bi)
                pre_b(bi, sfx, sub)

            def pre_seq(items):
                a = [(lambda it=it: pre_a(it[0], it[1], it[2])) for it in items]
                b = [(lambda it=it: pre_b(it[2], it[3], it[4])) for it in items]
                out = []
                n = len(items)
                for k in range(n + 1):
                    if k < n:
                        out.append(a[k])
                    if k >= 1:
                        out.append(b[k - 1])
                return out

            def pre_a(src_ap, srcB, bi):
                xt, xtB = next_xs()
                dma("sp", xt[:], src_ap, [srcB] if srcB is not None else [], [xtB], xtB)
                sm = small[:, bi * 4:bi * 4 + 4]
                smB = smallB[bi]
                for q4 in range(4):
                    jt, jtB = next_Tb()
                    S.op("act", (lambda e, jt=jt, q4=q4, xt=xt, sm=sm: e.activation(
                        out=jt[:], in_=xt[:, q4 * 512:(q4 + 1) * 512], func=AF.Square, accum_out=sm[:, q4:q4 + 1])),
                        reads=[xtB], writes=[jtB, smB] if q4 == 0 else [jtB], pwrites=[] if q4 == 0 else [smB])
                s3 = small3[:, bi:bi + 1]
                r3 = small3[:, 4 + bi:5 + bi]
                S.op("dve", (lambda e, s3=s3, sm=sm: e.tensor_reduce(out=s3, in_=sm, axis=AX.X, op=ALU.add)),
                     reads=[smB], writes=[sm3B[bi]])
                act(s3, s3, AF.Sqrt, [sm3B[bi], constB], [sm3B[bi]], scale=1.0 / D, bias=epsT[:])
                dve_recip(r3, s3, [sm3B[bi]], [sm3B[bi]])
                xn, xnB = next_xn()
                dve_ts(xn[:], xt[:], r3, None, ALU.mult, None, [xtB, sm3B[bi]], [xnB])
                pre_state[bi] = (xn, xnB)

            def pre_b(bi, sfx, sub):
                xn, xnB = pre_state[bi]
                vA = VIDX["A%d%s" % (sub, sfx)]
                vBi = VIDX["B%d%s" % (sub, sfx)]
                for hb in range(2):
                    bank = hb
                    pv = PS[bank][:].bitcast(BF16).rearrange("p (c t) -> p c t", c=8)
                    transposes([(pv[:, c, :], xn[:, (hb * 8 + c) * 128:(hb * 8 + c + 1) * 128]) for c in range(8)],
                               ident_bf[:], [xnB, constB], writes=[PB[bank]])
                    for c in range(8):
                        cc = hb * 8 + c
                        wr = [hTB[bi]] if (hb == 0 and c == 0) else []
                        pw = [] if (hb == 0 and c == 0) else [hTB[bi]]
                        S.op("act", (lambda e, pv=pv, c=c, cc=cc: e.activation(
                            out=hT[:, cc, bi * 128:(bi + 1) * 128], in_=pv[:, c, :], func=AF.Identity,
                            scale=vecs[:, vA, cc:cc + 1], bias=vecs[:, vBi, cc:cc + 1])),
                            reads=[PB[bank], vecB[vA], vecB[vBi]], writes=wr, pwrites=pw)

            def post_block(bi, nb_cols, src_ap, srcB, dst_ap, dstB, gi=0, banks=(2, 3, 4, 5)):
                xt, xtB = next_xs()
                dma("sp", xt[:], src_ap, [srcB] if srcB is not None else [], [xtB], xtB)
                sm = small[:, 16 + bi * 4:16 + bi * 4 + 4]
                smB = smallB[bi]
                jt, jtB = next_T()
                for q4 in range(4):
                    bank = banks[q4]
                    transposes([(PS[bank][:, c4 * 128:(c4 + 1) * 128], ystage[:, q4 * 4 + c4, bi * 128:(bi + 1) * 128]) for c4 in range(4)],
                               ident_f[:], [ysB[q4 * 4 + c4] for c4 in range(4)] + [constB], writes=[PB[bank]])
                    S.op("act", (lambda e, bank=bank, q4=q4, jt=jt, sm=sm: e.activation(out=jt[:], in_=PS[bank][:], func=AF.Square,
                                                                                     accum_out=sm[:, q4:q4 + 1])),
                         reads=[PB[bank]], writes=[jtB, smB] if q4 == 0 else [jtB], pwrites=[] if q4 == 0 else [smB])
                st2 = small2[:, bi:bi + 1]
                S.op("dve", lambda e: e.tensor_reduce(out=st2, in_=sm, axis=AX.X, op=ALU.add), reads=[smB], writes=[smB])
                act(st2, st2, AF.Sqrt, [smB, constB], [smB], scale=1.0 / D, bias=epsT[:])
                dve_recip(st2, st2, [smB], [smB])
                for q4 in range(4):
                    bank = banks[q4]
                    cols = slice(q4 * 512, (q4 + 1) * 512)
                    tt, ttB = next_T()
                    dve_stt(tt[:], PS[bank][:], st2, Gt[gi][:, cols], ALU.mult, ALU.mult, [PB[bank], smB, GB[gi]], [ttB])
                    if q4 == 0:
                        dve_tt(xt[:, cols], xt[:, cols], tt[:], ALU.add, [ttB, xtB], [xtB])
                    else:
                        S.op("dve", (lambda e, cols=cols, tt=tt: e.tensor_tensor(out=xt[:, cols], in0=xt[:, cols], in1=tt[:], op=ALU.add)),
                             reads=[ttB], pwrites=[xtB])
                dma("sp", dst_ap, xt[:], [xtB], [dstB], dstB)
                if DEBUG and dst_ap.tensor.name in ("x1", "x2"):
                    dd = x1d if dst_ap.tensor.name == "x1" else x2d
                    if not dbgB:
                        dbgB.append(S.buf("dbg", dma=True))
                    db = dbgB[0]
                    r0 = dst_ap.offset // D
                    S.op("sp", (lambda e, dd=dd, r0=r0, xt=xt: e.dma_start(out=dd[r0:r0 + 128, :], in_=xt[:])), reads=[xtB], pwrites=[db], dma=db)

            def ffn(l, sub, tiles, src_of, dst_of, tail_sec=None, skip_first_pre=False):
                wi = w_ffn_in[l].rearrange("(kc p) n -> p kc n", p=128)
                wo = w_ffn_out[l].rearrange("(f p) n -> p f n", p=128)
                cur_g = [None]

                def need_g(sfx):
                    nm = "G%d%s" % (sub, sfx)
                    if cur_g[0] != nm:
                        expand_vec(nm)
                        cur_g[0] = nm

                def pre_ops(blocks):
                    items = []
                    for bi, blk in enumerate(blocks):
                        sap, sB, sfx = src_of(blk)
                        items.append((sap, sB, bi, sfx, sub))
                    return pre_seq(items)

                def post_ops(blocks):
                    def mk(bi, blk):
                        def f():
                            sap, sB, sfx = src_of(blk)
                            dap, dB = dst_of(blk)
                            need_g(sfx)
                            post_block(bi, len(blocks) * 128, sap, sB, dap, dB, 0, banks=(0, 1, 6, 7))
                        return f
                    return [mk(bi, blk) for bi, blk in enumerate(blocks)]

                def A_ops(blocks):
                    nb = len(blocks)
                    tt_ = nb * 128
                    hreads = [hTB[bi] for bi in range(nb)]

                    def mk(fp):
                        def f():
                            sg_, sgB = load_w(wi[:, :, fp * 256:(fp + 1) * 256])
                            su_, suB = load_w(wi[:, :, DFF + fp * 256:DFF + (fp + 1) * 256])
                            for s2 in range(2):
                                ff = fp * 2 + s2
                                bg = 2 + (ff % 2) * 2
                                bu = bg + 1
                                mm(PS[bg][:, 0:tt_], [(sg_[:, kc, s2 * 128:(s2 + 1) * 128], hT[:, kc, 0:tt_]) for kc in range(16)],
                                   [sgB] + hreads, writes=[PB[bg]])
                                mm(PS[bu][:, 0:tt_], [(su_[:, kc, s2 * 128:(s2 + 1) * 128], hT[:, kc, 0:tt_]) for kc in range(16)],
                                   [suB] + hreads, writes=[PB[bu]])
                                tt, ttB = next_T()
                                act(tt[:, 0:tt_], PS[bg][:, 0:tt_], AF.Silu, [PB[bg]], [ttB])
                                dve_tt(h1[:, ff, 0:tt_], tt[:, 0:tt_], PS[bu][:, 0:tt_], ALU.mult, [ttB, PB[bu]], [h1B[ff]])
                        return f
                    return [mk(fp) for fp in range(22)]

                def B_ops(blocks):
                    nb = len(blocks)
                    tt_ = nb * 128

                    def mk(cp):
                        def f():
                            for j, (f0, nk) in enumerate(((0, 16), (16, 16), (32, 12))):
                                sl, slB = load_w(wo[:, f0:f0 + nk, cp * 256:(cp + 1) * 256], nk)
                                for s2 in range(2):
                                    by = 6 + s2
                                    pairs = [(sl[:, fi, s2 * 128:(s2 + 1) * 128], h1[:, f0 + fi, 0:tt_]) for fi in range(nk)]
                                    mm(PS[by][:, 0:tt_], pairs, [slB] + h1B[f0:f0 + nk],
                                       writes=[PB[by]] if j == 0 else (), pwrites=() if j == 0 else [PB[by]],
                                       start=(j == 0), stop=(j == 2))
                            for s2 in range(2):
                                c = cp * 2 + s2
                                act(ystage[:, c, 0:tt_], PS[6 + s2][:, 0:tt_], AF.Copy, [PB[6 + s2]], [ysB[c]])
                            if ffn_extra:
                                ffn_extra.pop(0)()
                        return f
                    return [mk(cp) for cp in range(8)]

                def interleave(prim, sec, frac=1.0):
                    n, m = len(prim), len(sec)
                    pos = {}
                    for k in range(m):
                        p = min(n - 1, int((k + 0.5) * n * frac / m))
                        pos.setdefault(p, []).append(sec[k])
                    for i, op_ in enumerate(prim):
                        op_()
                        for o in pos.get(i, []):
                            o()

                if not skip_first_pre:
                    for o in pre_ops(tiles[0]):
                        o()
                for t, blocks in enumerate(tiles):
                    if t == 0:
                        _ck('pre')
                    interleave(A_ops(blocks), post_ops(tiles[t - 1]) if t > 0 else [], 0.7)
                    if t == 0:
                        _ck('phaseA')
                    interleave(B_ops(blocks), pre_ops(tiles[t + 1]) if t + 1 < len(tiles) else (tail_sec or []), 1.0)
                    if t == 0:
                        _ck('phaseB')
                for o in post_ops(tiles[-1]):
                    o()


            def blkrows(ap, b):
                return ap[b * 128:(b + 1) * 128, :]

            tiles0 = [[0, 1, 2, 3], [4, 5, 6, 7], [8, 9, 10, 11], [12, 13, 14, 15], [16, 17, 18]]
            kvA = [4, 8, 12]
            ffn(0, 0, tiles0,
                lambda b: (blkrows(xe, b), None, "c" if b >= CTX0 else "x"),
                lambda b: (blkrows(x1, b), x1B[b]),
                tail_sec=pre_seq([(blkrows(x1, blk), x1B[blk], bi, "x", 1) for bi, blk in enumerate(kvA)]))
            _ck('ffn0')
            pass

            w_in_v = w_in.rearrange("(kc p) n -> p kc n", p=128)

            def kv_compute(blocks, wk_, wkB, wv_, wvB, skip=()):
                nb = len(blocks)
                tt_ = nb * 128
                hreads = [hTB[bi] for bi in range(nb)]
                for g in range(2):
                    bk = 2 + g
                    mm(PS[bk][:, 0:tt_], [(wk_[:, kc, g * 128:(g + 1) * 128], hT[:, kc, 0:tt_]) for kc in range(16)],
                       [wkB] + hreads, writes=[PB[bk]])
                    kb_, kbB = next_Tb()
                    dve_copy(kb_[:, 0:tt_], PS[bk][:, 0:tt_], [PB[bk]], [kbB])
                    br = 4 + g
                    mm(PS[br][:, 0:tt_], [(rmat[:], kb_[:, 0:tt_])], [kbB, cB2], writes=[PB[br]])
                    for bi, blk in enumerate(blocks):
                        if blk in skip:
                            continue
                        cs = slice(bi * 128, (bi + 1) * 128)
                        dst = kT_all[:, g, blk * 128:(blk + 1) * 128]
                        wr = [kTB[blk]] if g == 0 else []
                        pw = [] if g == 0 else [kTB[blk]]
                        if blk < CTX0:
                            t1, t1B = next_T()
                            t2, t2B = next_T()
                            dve_tt(t1[:, 0:128], PS[bk][:, cs], cos_sb[:, cs], ALU.mult, [PB[bk], csB], [t1B])
                            dve_tt(t2[:, 0:128], PS[br][:, cs], sin_sb[:, cs], ALU.mult, [PB[br], csB], [t2B])
                            S.op("dve", (lambda e, dst=dst, t1=t1, t2=t2: e.tensor_tensor(out=dst, in0=t1[:, 0:128], in1=t2[:, 0:128], op=ALU.add)),
                                 reads=[t1B, t2B], writes=wr, pwrites=pw)
                        else:
                            S.op("dve", (lambda e, dst=dst, bk=bk, cs=cs: e.tensor_copy(out=dst, in_=PS[bk][:, cs])),
                                 reads=[PB[bk]], writes=wr, pwrites=pw)
                for bi, blk in enumerate(blocks):
                    if blk in skip:
                        continue
                    bv = 6 + (bi % 2)
                    mm(PS[bv][:, 0:256], [(hT[:, kc, bi * 128:(bi + 1) * 128], wv_[:, kc, :]) for kc in range(16)],
                       [wvB, hTB[bi]], writes=[PB[bv]])
                    act(v_all[:, blk, :], PS[bv][:, 0:256], AF.Copy, [PB[bv]], [vB[blk]])

            wk0, wk0B = load_w(w_in_v[:, :, 1024:1280])
            wv0, wv0B = load_w(w_in_v[:, :, 1280:1536])
            pre_done = set()
            for bi, blk in enumerate(kvA):
                S.op("sp", (lambda e, bi=bi, blk=blk: e.dma_start(out=cos_sb[:, bi * 128:(bi + 1) * 128], in_=cosT[:, blk * 128:(blk + 1) * 128])),
                     writes=[csB] if bi == 0 else (), pwrites=() if bi == 0 else [csB], dma=csB)
                S.op("sp", (lambda e, bi=bi, blk=blk: e.dma_start(out=sin_sb[:, bi * 128:(bi + 1) * 128], in_=sinT[:, blk * 128:(blk + 1) * 128])),
                     pwrites=[csB], dma=csB)
            kv_compute(kvA, wk0, wk0B, wv0, wv0B)
            pre_done.update(kvA)
            for blocks in ([16, 17, 18],):
                nb = len(blocks)
                for o_ in pre_seq([(blkrows(x1, blk), x1B[blk], bi, "c" if blk >= CTX0 else "x", 1) for bi, blk in enumerate(blocks)]):
                    o_()
                t0 = blocks[0] * 128
                nrope = sum(1 for b in blocks if b < CTX0)
                dma("sp", cos_sb[:, 0:nrope * 128], cosT[:, t0:t0 + nrope * 128], [], [csB], csB)
                S.op("sp", (lambda e, t0=t0, nrope=nrope: e.dma_start(out=sin_sb[:, 0:nrope * 128], in_=sinT[:, t0:t0 + nrope * 128])),
                     pwrites=[csB], dma=csB)
                kv_compute(blocks, wk0, wk0B, wv0, wv0B)
                pre_done.update(blocks)

            _ck('kv')
            tilesM = [[0, 1, 2, 3], [4, 5, 6, 7], [8, 9, 10, 11], [12, 13, 14, 15]]
            lng = lngb_sb[:, 0:8]
            lnb = lngb_sb[:, 8:16]
            pend_post = []

            def run_post(k=1):
                for _ in range(k):
                    if pend_post:
                        pend_post.pop(0)()

            for ti, blocks in enumerate(tilesM):
                nb = 4
                tt_ = 512
                t0 = blocks[0] * 128
                if ti == 0:
                    for o_ in pre_seq([(blkrows(x1, blk), x1B[blk], bi, "x", 1) for bi, blk in enumerate(blocks)]):
                        o_()
                if ti == 1:
                    expand_vec("G1x")
                hreads = [hTB[bi] for bi in range(4)]
                dma("sp", cos_sb[:], cosT[:, t0:t0 + 512], [], [csB], csB)
                S.op("sp", (lambda e, t0=t0: e.dma_start(out=sin_sb[:], in_=sinT[:, t0:t0 + 512])), pwrites=[csB], dma=csB)

                def proj_chunk(sl, slB, s2, bank):
                    mm(PS[bank][:], [(sl[:, kc, s2 * 128:(s2 + 1) * 128], hT[:, kc, :]) for kc in range(16)],
                       [slB] + hreads, writes=[PB[bank]])

                pend_rot = []

                def q_finish(h, bq, br, qb_, qbB, t1, t1B):
                    mm(PS[br][:], [(rmat[:], qb_[:])], [qbB, cB2], writes=[PB[br]])
                    t2, t2B = next_T()
                    dve_tt(t2[:], PS[br][:], sin_sb[:], ALU.mult, [PB[br], csB], [t2B])
                    dve_tt(qT[:, h, :], t1[:], t2[:], ALU.add, [t1B, t2B], [qTB[h]])

                for cp in range(4):
                    sl, slB = load_w(w_in_v[:, :, cp * 256:(cp + 1) * 256])
                    for s2 in range(2):
                        h = cp * 2 + s2
                        bq = 2 + (h % 2)
                        br = 4 + (h % 2)
                        proj_chunk(sl, slB, s2, bq)
                        qb_, qbB = next_Tb()
                        dve_copy(qb_[:], PS[bq][:], [PB[bq]], [qbB])
                        t1, t1B = next_T()
                        dve_tt(t1[:], PS[bq][:], cos_sb[:], ALU.mult, [PB[bq], csB], [t1B])
                        if pend_rot:
                            pend_rot.pop(0)()
                        pend_rot.append(lambda h=h, bq=bq, br=br, qb_=qb_, qbB=qbB, t1=t1, t1B=t1B: q_finish(h, bq, br, qb_, qbB, t1, t1B))
                    while pend_rot:
                        pend_rot.pop(0)()
                    run_post(1)
                wkt, wktB = load_w(w_in_v[:, :, 1024:1280])
                wvt, wvtB = load_w(w_in_v[:, :, 1280:1536])
                kv_compute(blocks, wkt, wktB, wvt, wvtB, skip=pre_done)
                run_post(4)
                _ck('mq')
                vgf = ystage
                pend_stats = []
                for cp in range(4):
                    sl, slB = load_w(w_in_v[:, :, 2560 + cp * 256:2560 + (cp + 1) * 256])
                    for s2 in range(2):
                        c = cp * 2 + s2
                        bq = 2 + (c % 2)
                        proj_chunk(sl, slB, s2, bq)
                        act(vgf[:, c, :], PS[bq][:], AF.Gelu_apprx_tanh, [PB[bq]], [ysB[c]])
                        sq, sqB = next_T()
                        act(sq[:], vgf[:, c, :], AF.Square, [ysB[c]], [sqB])
                        if pend_stats:
                            pend_stats.pop(0)()

                        def stats(c=c, sq=sq, sqB=sqB):
                            mm(PS[6][:], [(ones_f[:], vgf[:, c, :])], [ysB[c], constB], writes=[PB[6]] if c == 0 else (),
                               pwrites=() if c == 0 else [PB[6]], start=(c == 0), stop=(c == 7))
                            mm(PS[7][:], [(ones_f[:], sq[:])], [sqB, constB], writes=[PB[7]] if c == 0 else (),
                               pwrites=() if c == 0 else [PB[7]], start=(c == 0), stop=(c == 7))
                        pend_stats.append(stats)
                while pend_stats:
                    pend_stats.pop(0)()
                mu = cos_sb
                var = sin_sb
                act(mu[:], PS[6][:], AF.Identity, [PB[6]], [csB], scale=1.0 / 1024)
                msq, msqB = next_T()
                dve_tt(msq[:], mu[:], mu[:], ALU.mult, [csB], [msqB])
                S.op("dve", lambda e, msq=msq, var=var: e.scalar_tensor_tensor(out=var[:], in0=PS[7][:], scalar=1.0 / 1024, in1=msq[:],
                                                             op0=ALU.mult, op1=ALU.subtract),
                     reads=[PB[7], msqB], pwrites=[csB])
                S.op("act", lambda e, var=var: e.activation(out=var[:], in_=var[:], func=AF.Sqrt, bias=epsT[:]),
                     reads=[csB, constB], pwrites=[csB])
                S.op("dve", lambda e, var=var: e.reciprocal(out=var[:], in_=var[:]), reads=[csB], pwrites=[csB])
                _ck('mvg')

                u_state = {}

                def u_chunk(c):
                    if c % 2 == 0:
                        u_state["sl"] = load_w(w_in_v[:, :, 1536 + (c // 2) * 256:1536 + (c // 2 + 1) * 256])
                    sl, slB = u_state["sl"]
                    bq = c % 2
                    proj_chunk(sl, slB, c % 2, bq)
                    act(uT[:, c, :], PS[bq][:], AF.Gelu_apprx_tanh, [PB[bq]], [uTB[c]])

                def norm_pair(c0):
                    for c in (c0, c0 + 1):
                        tt, ttB = next_T()
                        dve_tt(tt[:], vgf[:, c, :], mu[:], ALU.subtract, [ysB[c], csB], [ttB])
                        dve_tt(tt[:], tt[:], var[:], ALU.mult, [ttB, csB], [ttB])
                        act(vnT[:, c, :], tt[:], AF.Identity, [ttB, cB3], [vnTB[c]], scale=lng[:, c:c + 1], bias=lnb[:, c:c + 1])

                def att_unit(bi, qb, g, mid=None):
                    prev = (qb - 1, 0) if qb > 0 else (HALO, 2)
                    nxt = (qb + 1, 1) if qb < NOWN - 1 else (HALO, 3)
                    kbs = [(prev[0], prev[1]), (qb, None), (nxt[0], nxt[1]), (CTX0, None), (CTX0 + 1, None)]
                    qv = qT[:, 4 * g:4 * g + 4, bi * 128:(bi + 1) * 128]
                    qreads = qTB[4 * g:4 * g + 4]
                    bO, bD = 5, 6
                    sbanks = [2, 3, 4]

                    def s_mm(i):
                        kb, mi = kbs[i]
                        bank = sbanks[i % 3]
                        pairs = [(kT_all[:, g, kb * 128:(kb + 1) * 128], qv)]
                        if mi is not None:
                            pairs.append((ident_bf[:], mask_sb[:, mi, :].unsqueeze(1).to_broadcast([128, 4, 128])))
                        mm(PS[bank][:].rearrange("p (h t) -> p h t", h=4), pairs,
                           [kTB[kb], cB2, constB] + qreads, writes=[PB[bank]])
                    s_mm(0)
                    s_mm(1)
                    s_mm(2)
                    if mid is not None:
                        mid()
                    for i in range(5):
                        kb, mi = kbs[i]
                        bank = sbanks[i % 3]
                        pT, pTB = next_Tb()
                        act(pT[:], PS[bank][:], AF.Exp, [PB[bank]], [pTB], scale=SCALE)
                        mm(PS[bO][:], [(v_all[:, kb, g * 128:(g + 1) * 128], pT[:])], [vB[kb], pTB],
                           writes=[PB[bO]] if i == 0 else (), pwrites=() if i == 0 else [PB[bO]],
                           start=(i == 0), stop=(i == 4))
                        pairsD = [(ones_bf[:], pT[:].rearrange("p (h t) -> p h t", h=4))]
                        if i == 4:
                            pairsD.append((ones_bf[0:1, :], es_hi[0:1, 4 * g:4 * g + 4].unsqueeze(2).to_broadcast([1, 4, 128])))
                            pairsD.append((ones_bf[0:1, :], es_lo[0:1, 4 * g:4 * g + 4].unsqueeze(2).to_broadcast([1, 4, 128])))
                        mm(PS[bD][:].rearrange("p (h t) -> p h t", h=4), pairsD, [pTB, constB, esB],
                           writes=[PB[bD]] if i == 0 else (), pwrites=() if i == 0 else [PB[bD]],
                           start=(i == 0), stop=(i == 4))
                        if i + 3 < 5:
                            s_mm(i + 3)
                    rD, rDB = next_T()
                    dve_recip(rD[:], PS[bD][:], [PB[bD]], [rDB])
                    ov = mbT[:, 4 * g:4 * g + 4, bi * 128:(bi + 1) * 128]
                    S.op("dve", (lambda e, ov=ov, rD=rD, bO=bO: e.tensor_tensor(
                        out=ov, in0=PS[bO][:].rearrange("p (h t) -> p h t", h=4),
                        in1=rD[:].rearrange("p (h t) -> p h t", h=4), op=ALU.mult)),
                        reads=[PB[bO], rDB], writes=[mbTB[bi][g]])

                units = [(bi, qb, g) for bi, qb in enumerate(blocks) for g in range(2)]
                for ui, (bi, qb, g) in enumerate(units):
                    att_unit(bi, qb, g, mid=(lambda ui=ui: u_chunk(ui)))
                    if ui % 2 == 0:
                        norm_pair(ui)
                _ck('matt')
                for bi in range(4):
                    bank = bi % 2
                    pv = PS[bank][:].bitcast(BF16)
                    transposes([(pv[:, c * 128:(c + 1) * 128], vnT[:, c, bi * 128:(bi + 1) * 128]) for c in range(8)],
                               ident_bf[:], vnTB + [constB], writes=[PB[bank]])
                    dve_copy(vn_tm[:, bi, :], pv[:, :], [PB[bank]], [vntmB[bi]])
                for bi in range(4):
                    for hb in range(2):
                        bank = 4 + hb
                        for g4 in range(4):
                            g = hb * 4 + g4
                            o = PS[bank][:, g4 * 128:(g4 + 1) * 128]
                            pairs = [(vn_tm[:, bi, g * 128:(g + 1) * 128], wsT_sb[:, g, :]),
                                     (ones_bf[0:1, :], bs_sb[0:1, g * 128:(g + 1) * 128])]
                            if g4 == 0:
                                mm(o, pairs, [vntmB[bi], cB2, constB], writes=[PB[bank]])
                            else:
                                mm(o, pairs, [vntmB[bi], cB2, constB], pwrites=[PB[bank]])
                        uv = uT[:, hb * 4:hb * 4 + 4, bi * 128:(bi + 1) * 128]
                        S.op("dve", (lambda e, uv=uv, bank=bank: e.tensor_tensor(
                            out=uv, in0=uv, in1=PS[bank][:].rearrange("p (g t) -> p g t", g=4), op=ALU.mult)),
                            reads=[PB[bank]] + uTB[hb * 4:hb * 4 + 4], pwrites=uTB[hb * 4:hb * 4 + 4])
                mbreads = [mbTB[bi][g] for bi in range(4) for g in range(2)]
                _ck('msp')
                if blocks[0] == 0:
                    dump("d_q", qT, qTB)
                    dump("d_k", kT_all, kTB)
                    dump("d_v", v_all, vB)
                    dump("d_h", hT[:], hreads)
                wa_v = w_bra.rearrange("(kc p) n -> p kc n", p=128)
                wb_v = w_brb.rearrange("(kc p) n -> p kc n", p=128)
                for cp in range(8):
                    sga, sgaB = load_w(w_in_v[:, :, 3584 + cp * 256:3584 + (cp + 1) * 256])
                    sgb, sgbB = load_w(w_in_v[:, :, 5632 + cp * 256:5632 + (cp + 1) * 256])
                    for s2 in range(2):
                        c = cp * 2 + s2
                        if s2 == 0:
                            swa, swaB = load_w(wa_v[:, :, cp * 256:(cp + 1) * 256], 8)
                            swb, swbB = load_w(wb_v[:, :, cp * 256:(cp + 1) * 256], 8)
                        b_ga, b_gb, b_pa, b_pb = (2, 3, 4, 5) if c % 2 == 0 else (6, 7, 0, 1)
                        proj_chunk(sga, sgaB, s2, b_ga)
                        proj_chunk(sgb, sgbB, s2, b_gb)
                        mm(PS[b_pa][:], [(swa[:, kc, s2 * 128:(s2 + 1) * 128], uT[:, kc, :]) for kc in range(8)],
                           [swaB] + uTB, writes=[PB[b_pa]])
                        mm(PS[b_pb][:], [(swb[:, kc, s2 * 128:(s2 + 1) * 128], mbT[:, kc, :]) for kc in range(8)],
                           [swbB] + mbreads, writes=[PB[b_pb]])
                        ta, taB = next_T()
                        tb, tbB = next_T()
                        act(ta[:], PS[b_ga][:], AF.Sigmoid, [PB[b_ga]], [taB])
                        act(tb[:], PS[b_gb][:], AF.Sigmoid, [PB[b_gb]], [tbB])
                        dve_tt(ta[:], ta[:], PS[b_pa][:], ALU.mult, [taB, PB[b_pa]], [taB])
                        dve_tt(tb[:], tb[:], PS[b_pb][:], ALU.mult, [tbB, PB[b_pb]], [tbB])
                        dve_tt(mgT[:, c, :], ta[:], tb[:], ALU.add, [taB, tbB], [mgTB[c]] + vnTB + vntmB)
                if blocks[0] == 0:
                    dump("d_ma", uT, uTB)
                    dump("d_mb", mbT, mbreads)
                    dump("d_mg", mgT, mgTB)
                _ck('mmerge')
                nxt_blocks = tilesM[ti + 1] if ti + 1 < len(tilesM) else []
                if nxt_blocks:
                    nxt_pre = pre_seq([(blkrows(x1, b_), x1B[b_], nbi, "x", 1) for nbi, b_ in enumerate(nxt_blocks)])
                else:
                    nxt_pre = pre_seq([(blkrows(x2, b_), x2B[b_], nbi, "x", 2) for nbi, b_ in enumerate(tilesM[0])])
                for cp in range(8):
                    sl, slB = load_w(w_o.rearrange("(kc p) n -> p kc n", p=128)[:, :, cp * 256:(cp + 1) * 256])
                    for s2 in range(2):
                        c = cp * 2 + s2
                        by = 6 + (c % 2)
                        mm(PS[by][:], [(sl[:, kc, s2 * 128:(s2 + 1) * 128], mgT[:, kc, :]) for kc in range(16)],
                           [slB] + mgTB, writes=[PB[by]])
                        act(ystage[:, c, :], PS[by][:], AF.Copy, [PB[by]], [ysB[c]])
                    if nxt_pre:
                        nxt_pre.pop(0)()
                    if cp % 2 == 0 and deferred_mod:
                        deferred_mod.pop(0)()
                while nxt_pre:
                    nxt_pre.pop(0)()
                for bi, blk in enumerate(blocks):
                    pend_post.append(lambda bi=bi, blk=blk: post_block(bi, 512, blkrows(x1, blk), x1B[blk], blkrows(x2, blk), x2B[blk],
                                                                        0, banks=(0, 1, 6, 7)))
                _ck('mtile0')
            while deferred_mod:
                deferred_mod.pop(0)()
            run_post(8)
            _ck('mixer')
            pass

            ffn(1, 2, tilesM,
                lambda b: (blkrows(x2, b), x2B[b], "x"),
                lambda b: (blkrows(outp, b), outB[b]), skip_first_pre=True)


        except _Stop:
            pass
        S.wait_all("sp", outB + x1B + x2B + dbgB)
        S.emit()
    return nc


def _rope_tables(positions):
    pos = np.asarray(positions, dtype=np.float32)
    row = np.floor(pos / 64.0).astype(np.float32)
    col = (pos - row * 64.0).astype(np.float32)
    freqs = (np.float32(10000.0) ** (-np.arange(32, dtype=np.float32) / np.float32(32))).astype(np.float32)
    ang_r = row[:, None] * freqs[None, :]
    ang_c = col[:, None] * freqs[None, :]
    ang = np.concatenate([ang_r, ang_r, ang_c, ang_c], axis=-1).astype(np.float32)
    return np.ascontiguousarray(np.cos(ang).T.astype(np.float32)), np.ascontiguousarray(np.sin(ang).T.astype(np.float32))


def _const_tables(half):
    RT = np.zeros((128, 128), np.float32)
    for m in range(128):
        if (m % 64) < 32:
            RT[m + 32, m] = -1.0
        else:
            RT[m - 32, m] = 1.0
    NEG = -30000.0
    kj = np.arange(128)[:, None]
    qi = np.arange(128)[None, :]
    tri_prev = np.where(kj >= qi, 0.0, NEG).astype(np.float32)
    tri_next = np.where(kj <= qi, 0.0, NEG).astype(np.float32)
    allm = np.full((128, 128), NEG, np.float32)
    m_first = allm if half == 0 else tri_prev
    m_last = tri_next if half == 0 else allm
    ms = np.stack([m for m in (tri_prev, tri_next, m_first, m_last)], axis=1)
    return RT, np.ascontiguousarray(ms.astype(np.float32))


_NC_CACHE = {}


def prep_inputs(x, c, ctx, c_ctx, w_ada, b_ada, norm_g, w_ffn_in, w_ffn_out, w_in,
                gmlp_ln_g, gmlp_ln_b, w_spatial, b_spatial, attn_sink,
                w_branch_a, w_branch_b, w_out, cores=range(8)):
    f = lambda a: np.ascontiguousarray(np.asarray(a, dtype=np.float32))
    x, c, ctx, c_ctx = f(x), f(c), f(ctx), f(c_ctx)
    shared = {
        "w_ada": f(w_ada)[0],
        "bpp": np.ascontiguousarray(f(b_ada)[0].reshape(144, 128).T),
        "gpp": np.ascontiguousarray(f(norm_g)[0].reshape(96, 128).T),
        "w_ffn_in": f(w_ffn_in)[0], "w_ffn_out": f(w_ffn_out)[0], "w_in": f(w_in)[0],
        "lngb": np.ascontiguousarray(np.concatenate([f(gmlp_ln_g)[0].reshape(8, 128).T, f(gmlp_ln_b)[0].reshape(8, 128).T], axis=1)),
        "w_sT": np.ascontiguousarray(f(w_spatial)[0].transpose(2, 0, 1)),
        "b_sp": f(b_spatial)[0].reshape(1, 1024),
        "sink": f(attn_sink)[0].reshape(1, 8),
        "w_bra": f(w_branch_a)[0], "w_brb": f(w_branch_b)[0], "w_o": f(w_out)[0],
    }
    in_maps = []
    for core in cores:
        b, half = core // 2, core % 2
        own = np.arange(half * 2048, (half + 1) * 2048)
        halo = np.arange(2048, 2176) if half == 0 else np.arange(1920, 2048)
        xe = np.concatenate([x[b, own], x[b, halo], ctx[b]], axis=0)
        cosT, sinT = _rope_tables(np.concatenate([own, halo]))
        RT, ms = _const_tables(half)
        cvec = np.concatenate([c[b].reshape(16, 128).T, c_ctx.reshape(16, 128).T], axis=1)
        m = dict(shared)
        m.update({"xe": np.ascontiguousarray(xe), "cvec": np.ascontiguousarray(cvec), "cosT": cosT, "sinT": sinT,
                  "rmatT": RT, "masks": ms})
        in_maps.append(m)
    return in_maps


def kernel(**inputs):
    in_maps = prep_inputs(**inputs)
    if "nc" not in _NC_CACHE:
        _NC_CACHE["nc"] = build_program()
    res = run_bass_kernel_spmd(_NC_CACHE["nc"], in_maps, core_ids=list(range(8)))
    out = np.empty((4, 4096, 2048), np.float32)
    for core in range(8):
        b, half = core // 2, core % 2
        out[b, half * 2048:(half + 1) * 2048] = res.results[core]["out"]
    return out
```

```python
import contextlib
import numpy as np
import concourse.bass as bass
import concourse.mybir as mybir
from concourse.bass_utils import run_bass_kernel_spmd

F32 = mybir.dt.float32
BF16 = mybir.dt.bfloat16
AF = mybir.ActivationFunctionType
ALU = mybir.AluOpType
AX = mybir.AxisListType

D = 2048
DFF = 5632
NOWN = 16
NBLK = 19
HALO = 16
CTX0 = 17
EPS = 1e-6
SCALE = 128.0 ** -0.5
NSLOT = 6
STOP = None
DEBUG = False


class _Stop(Exception):
    pass


def _ck(name):
    if STOP == name:
        raise _Stop()


class Buf:
    __slots__ = ("name", "w", "r", "prev_r", "dsem", "dval")

    def __init__(self, name):
        self.name = name
        self.w = {}
        self.r = {}
        self.prev_r = {}
        self.dsem = None
        self.dval = 0


def _merge(dst, src):
    for k, v in src.items():
        if dst.get(k, 0) < v:
            dst[k] = v


class Stream:
    def __init__(self, name, sem_id):
        self.name = name
        self.sem_id = sem_id
        self.count = 0
        self.ops = []
        self.seen = {}


class Sched:
    ENGS = ("pe", "act", "dve", "pool", "sp")

    def __init__(self, nc, stack):
        self.nc = nc
        self.stack = stack
        self.sems = []
        self.streams = {}
        for n in self.ENGS:
            self.streams[n] = Stream(n, self.new_sem("c_" + n))
        self.nbuf = 0

    def new_sem(self, name):
        h = self.stack.enter_context(self.nc.semaphore(name))
        self.sems.append(h)
        return len(self.sems) - 1

    def buf(self, name=None, dma=False):
        self.nbuf += 1
        b = Buf(name or f"b{self.nbuf}")
        if dma:
            b.dsem = self.new_sem("d%d" % self.nbuf)
        return b

    def op(self, eng, fn, reads=(), writes=(), pwrites=(), dma=None):
        st = self.streams[eng]
        raw = {}
        other = {}
        for b in reads:
            _merge(raw, b.w)
        for b in writes:
            _merge(other, b.r)
            if not b.r:
                _merge(other, b.w)
        for b in pwrites:
            _merge(other, b.prev_r)
        _merge(raw, other)
        waits = []
        for k, v in raw.items():
            if st.seen.get(k, 0) < v:
                st.seen[k] = v
                waits.append((k, v))
        if dma is not None:
            dma.dval += 16
            tok = (dma.dsem, dma.dval)
            inc = (dma.dsem, 16)
        else:
            st.count += 1
            tok = (st.sem_id, st.count)
            inc = (st.sem_id, 1)
        st.ops.append((waits, fn, inc))
        for b in reads:
            if b.r.get(tok[0], 0) < tok[1]:
                b.r[tok[0]] = tok[1]
        for b in writes:
            b.prev_r = b.r
            b.r = {}
            b.w = {tok[0]: tok[1]}
        for b in pwrites:
            if b.w.get(tok[0], 0) < tok[1]:
                b.w[tok[0]] = tok[1]
        return tok

    def wait_all(self, eng, bufs):
        st = self.streams[eng]
        deps = {}
        for b in bufs:
            _merge(deps, b.r)
            _merge(deps, b.w)
        waits = []
        for k, v in deps.items():
            if st.seen.get(k, 0) < v:
                st.seen[k] = v
                waits.append((k, v))
        st.ops.append((waits, None, None))

    def barrier(self):
        cur = {self.streams[n].sem_id: self.streams[n].count for n in ("pe", "act", "dve", "pool")}
        for n in self.ENGS:
            st = self.streams[n]
            waits = []
            for k, v in cur.items():
                if v > 0 and st.seen.get(k, 0) < v:
                    st.seen[k] = v
                    waits.append((k, v))
            st.ops.append((waits, None, None))

    def emit(self):
        nc = self.nc
        sems = self.sems

        def replay(st):
            def run(e):
                for waits, fn, inc in st.ops:
                    for k, v in waits:
                        e.wait_ge(sems[k], v)
                    if fn is not None:
                        ins = fn(e)
                        ins.then_inc(sems[inc[0]], inc[1])
            return run

        with nc.Block() as block:
            block.tensor(replay(self.streams["pe"]))
            block.scalar(replay(self.streams["act"]))
            block.vector(replay(self.streams["dve"]))
            block.gpsimd(replay(self.streams["pool"]))
            block.sync(replay(self.streams["sp"]))


def build_program():
    nc = bass.Bass("TRN2", target_bir_lowering=False)

    def din(name, shape):
        return nc.dram_tensor(name, shape, F32, kind="ExternalInput").ap()

    xe = din("xe", [NBLK * 128, D])
    cvec = din("cvec", [128, 32])
    w_ada = din("w_ada", [D, 9 * D])
    bpp_d = din("bpp", [128, 144])
    gpp_d = din("gpp", [128, 96])
    w_ffn_in = din("w_ffn_in", [2, D, 2 * DFF])
    w_ffn_out = din("w_ffn_out", [2, DFF, D])
    w_in = din("w_in", [D, 7680])
    lngb = din("lngb", [128, 16])
    w_sT = din("w_sT", [128, 8, 128])
    b_sp = din("b_sp", [1, 1024])
    sink = din("sink", [1, 8])
    w_bra = din("w_bra", [1024, D])
    w_brb = din("w_brb", [1024, D])
    w_o = din("w_o", [D, D])
    cosT = din("cosT", [128, 17 * 128])
    sinT = din("sinT", [128, 17 * 128])
    rmatT = din("rmatT", [128, 128])
    masks = din("masks", [128, 4, 128])
    x1 = nc.dram_tensor("x1", [NBLK * 128, D], F32).ap()
    x2 = nc.dram_tensor("x2", [NOWN * 128, D], F32).ap()
    x1d = x2d = None
    if DEBUG:
        x1d = nc.dram_tensor("x1d", [NBLK * 128, D], F32, kind="ExternalOutput").ap()
        x2d = nc.dram_tensor("x2d", [NOWN * 128, D], F32, kind="ExternalOutput").ap()
    dbgB = []
    dd = {}
    if DEBUG:
        for nm, shp in (("d_q", [128, 8, 512]), ("d_k", [128, 2, NBLK * 128]), ("d_v", [128, NBLK, 256]),
                        ("d_ma", [128, 8, 512]), ("d_mb", [128, 8, 512]), ("d_mg", [128, 16, 512]), ("d_h", [128, 16, 512])):
            dd[nm] = nc.dram_tensor(nm, shp, BF16, kind="ExternalOutput").ap()
    outp = nc.dram_tensor("out", [NOWN * 128, D], F32, kind="ExternalOutput").ap()

    with contextlib.ExitStack() as stk:
        S = Sched(nc, stk)

        def sb(name, shape, dt):
            return stk.enter_context(nc.sbuf_tensor(name, shape, dt))

        PS = [stk.enter_context(nc.psum_tensor(f"ps{i}", [128, 512], F32)) for i in range(8)]
        PB = [S.buf(f"ps{i}") for i in range(8)]

        slots = [sb(f"wsl{i}", [128, 16, 256], BF16) for i in range(NSLOT)]
        slotB = [S.buf(f"wsl{i}", dma=True) for i in range(NSLOT)]
        slot_ctr = [0]

        def next_slot():
            i = slot_ctr[0] % NSLOT
            slot_ctr[0] += 1
            return slots[i], slotB[i]

        Gt = [sb("G0", [128, D], F32)]
        GB = [S.buf("G0", dma=True)]
        xs = [sb(f"xs{i}", [128, D], F32) for i in range(2)]
        xsB = [S.buf(f"xs{i}", dma=True) for i in range(2)]
        xs_ctr = [0]

        def next_xs():
            i = xs_ctr[0] % 2
            xs_ctr[0] += 1
            return xs[i], xsB[i]

        xns = [sb(f"xn{i}", [128, D], BF16) for i in range(2)]
        xnsB = [S.buf(f"xn{i}") for i in range(2)]
        xn_ctr = [0]

        def next_xn():
            i = xn_ctr[0] % 2
            xn_ctr[0] += 1
            return xns[i], xnsB[i]

        hT = sb("hT", [128, 16, 512], BF16)
        hTB = [S.buf(f"hT{i}") for i in range(4)]
        ystage = sb("ystage", [128, 16, 512], F32)
        ysB = [S.buf(f"ys{i}") for i in range(16)]
        arena = sb("arena", [128, 26 * 1024], BF16)
        h1 = arena[:, 0:44 * 512].rearrange("p (f t) -> p f t", f=44)
        h1B = [S.buf(f"h1_{i}") for i in range(44)]
        def aview(off_kb, nchunk, t):
            o = off_kb * 512
            return arena[:, o:o + nchunk * t].rearrange("p (c t) -> p c t", c=nchunk)
        qT = aview(0, 8, 512)
        mbT = qT
        uT = aview(8, 8, 512)
        vn_tm = aview(16, 4, 1024)
        vnT = aview(24, 8, 512)
        mgT = aview(16, 16, 512)
        kT_all = arena[:, 32 * 512:32 * 512 + 2 * NBLK * 128].rearrange("p (g t) -> p g t", g=2)
        v_all = arena[:, 32 * 512 + 2 * NBLK * 128:32 * 512 + 4 * NBLK * 128].rearrange("p (b c) -> p b c", b=NBLK)
        qTB = [S.buf(f"qT{i}") for i in range(8)]
        uTB = [S.buf(f"uT{i}") for i in range(8)]
        vnTB = [S.buf(f"vnT{i}") for i in range(8)]
        vntmB = [S.buf(f"vntm{i}") for i in range(4)]
        mbTB = [[S.buf(f"mbT{b}_{g}") for g in range(2)] for b in range(4)]
        mgTB = [S.buf(f"mgT{i}") for i in range(16)]
        kTB = [S.buf(f"kT{i}") for i in range(NBLK)]
        vB = [S.buf(f"v{i}") for i in range(NBLK)]
        scx = arena[:, 0:2048].rearrange("p (k m) -> p k m", k=16)
        scc = arena[:, 2048:4096].rearrange("p (k m) -> p k m", k=16)
        bada_bf = arena[0:1, 4096:4096 + 2048]
        scB = S.buf("sc")
        badaB = S.buf("bada", dma=True)

        T = [sb(f"T{i}", [128, 512], F32) for i in range(4)]
        TB = [S.buf(f"T{i}") for i in range(4)]
        t_ctr = [0]

        def next_T():
            i = t_ctr[0] % 4
            t_ctr[0] += 1
            return T[i], TB[i]

        Tb = [sb(f"Tb{i}", [128, 512], BF16) for i in range(4)]
        TbB = [S.buf(f"Tb{i}") for i in range(4)]
        tb_ctr = [0]

        def next_Tb():
            i = tb_ctr[0] % 4
            tb_ctr[0] += 1
            return Tb[i], TbB[i]

        ident_f = sb("ident_f", [128, 128], F32)
        ident_bf = sb("ident_bf", [128, 128], BF16)
        ones_f = sb("ones_f", [128, 128], F32)
        ones_bf = sb("ones_bf", [128, 128], BF16)
        rmat = sb("rmat", [128, 128], BF16)
        mask_sb = sb("mask_sb", [128, 4, 128], BF16)
        wsT_sb = sb("wsT_sb", [128, 8, 128], BF16)
        bs_sb = sb("bs_sb", [1, 1024], BF16)
        es_hi = sb("es_hi", [1, 8], BF16)
        es_lo = sb("es_lo", [1, 8], BF16)
        sink_sb = sb("sink_sb", [1, 8], F32)
        lngb_sb = sb("lngb_sb", [128, 16], F32)
        cvec_sb = sb("cvec_sb", [128, 32], F32)
        sc_sb = sb("sc_sb", [128, 32], F32)
        sc_bf = sb("sc_bf", [128, 32], BF16)
        bpp = sb("bpp_sb", [128, 144], F32)
        gpp = sb("gpp_sb", [128, 96], F32)
        one11 = sb("one11", [1, 1], F32)
        NV = 14
        vecs = sb("vecs", [128, NV, 16], F32)
        vecB = [S.buf(f"vec{i}") for i in range(NV)]
        VIDX = {n: i for i, n in enumerate(["B0x", "A0x", "G0x", "B1x", "A1x", "G1x", "B2x", "A2x", "G2x",
                                            "B0c", "A0c", "G0c", "B1c", "A1c"])}
        cos_sb = sb("cos_sb", [128, 512], F32)
        sin_sb = sb("sin_sb", [128, 512], F32)
        csB = S.buf("cossin", dma=True)
        small = sb("small", [128, 32], F32)
        small2 = sb("small2", [128, 8], F32)
        small3 = sb("small3", [128, 8], F32)
        sm3B = [S.buf(f"sm3_{i}") for i in range(4)]
        smallB = [S.buf(f"small{i}") for i in range(4)]
        epsT = sb("epsT", [128, 1], F32)
        constB = S.buf("consts", dma=True)

        def act(out, in_, func, reads, writes, **kw):
            S.op("act", lambda e: e.activation(out=out, in_=in_, func=func, **kw), reads=reads, writes=writes)

        def dve_tt(out, in0, in1, op, reads, writes, eng="dve"):
            S.op(eng, lambda e: e.tensor_tensor(out=out, in0=in0, in1=in1, op=op), reads=reads, writes=writes)

        def dve_stt(out, in0, scalar, in1, op0, op1, reads, writes):
            S.op("dve", lambda e: e.scalar_tensor_tensor(out=out, in0=in0, scalar=scalar, in1=in1, op0=op0, op1=op1),
                 reads=reads, writes=writes)

        def dve_ts(out, in0, s1, s2, op0, op1, reads, writes):
            if op1 is None:
                S.op("dve", lambda e: e.tensor_scalar(out=out, in0=in0, scalar1=s1, scalar2=None, op0=op0),
                     reads=reads, writes=writes)
            else:
                S.op("dve", lambda e: e.tensor_scalar(out=out, in0=in0, scalar1=s1, scalar2=s2, op0=op0, op1=op1),
                     reads=reads, writes=writes)

        def dve_copy(out, in_, reads, writes):
            S.op("dve", lambda e: e.tensor_copy(out=out, in_=in_), reads=reads, writes=writes)

        def dve_recip(out, in_, reads, writes):
            S.op("dve", lambda e: e.reciprocal(out=out, in_=in_), reads=reads, writes=writes)

        def mm(out, pairs, reads, writes=(), pwrites=(), start=True, stop=True):
            pairs = list(pairs)

            def fn(e):
                n = len(pairs)
                ins = None
                for i, (l, r) in enumerate(pairs):
                    ins = e.matmul(out, lhsT=l, rhs=r, start=(start and i == 0), stop=(stop and i == n - 1))
                return ins
            S.op("pe", fn, reads=reads, writes=writes, pwrites=pwrites)

        def transposes(items, ident, reads, writes=(), pwrites=()):
            items = list(items)

            def fn(e):
                ins = None
                for o, i in items:
                    ins = e.transpose(o, i, ident)
                return ins
            S.op("pe", fn, reads=reads, writes=writes, pwrites=pwrites)

        def dma(q, out, in_, reads, writes, dbuf):
            S.op(q, lambda e: e.dma_start(out=out, in_=in_), reads=reads, writes=writes, dma=dbuf)

        def load_w(src_ap, nk=16):
            sl, slB = next_slot()
            dma("pool", sl[:, 0:nk, :], src_ap, [], [slB], slB)
            return sl, slB

        def dump(nm, src, rbufs):
            if not DEBUG:
                return
            if not dbgB:
                dbgB.append(S.buf("dbg", dma=True))
            db = dbgB[0]
            S.op("sp", lambda e: e.dma_start(out=dd[nm], in_=src), reads=rbufs, pwrites=[db], dma=db)

        S.op("pool", lambda e: e.memset(ident_f[:], 0.0), writes=[constB])
        S.op("pool", lambda e: e.affine_select(out=ident_f[:], in_=ident_f[:], pattern=[[-1, 128]],
                                               compare_op=ALU.not_equal, fill=1.0, base=0, channel_multiplier=1),
             reads=[constB], writes=[constB])
        S.op("dve", lambda e: e.tensor_copy(out=ident_bf[:], in_=ident_f[:]), reads=[constB], pwrites=[constB])
        S.op("dve", lambda e: e.memset(ones_f[:], 1.0), pwrites=[constB])
        S.op("dve", lambda e: e.memset(ones_bf[:], 1.0), pwrites=[constB])
        S.op("dve", lambda e: e.memset(epsT[:], EPS), pwrites=[constB])
        cB2 = S.buf("consts2", dma=True)
        for (dst, src) in ((rmat[:], rmatT), (mask_sb[:], masks), (wsT_sb[:], w_sT), (bs_sb[:], b_sp)):
            S.op("pool", (lambda e, dst=dst, src=src: e.dma_start(out=dst, in_=src)), pwrites=[cB2], dma=cB2)
        cB3 = S.buf("consts3", dma=True)
        S.op("dve", lambda e: e.memset(one11[:], 1.0), pwrites=[constB])
        for (dst, src) in ((sink_sb[:], sink), (lngb_sb[:], lngb), (cvec_sb[:], cvec), (bpp[:], bpp_d), (gpp[:], gpp_d)):
            S.op("sp", (lambda e, dst=dst, src=src: e.dma_start(out=dst, in_=src)), pwrites=[cB3], dma=cB3)
        esB = S.buf("es")
        act(sink_sb[:], sink_sb[:], AF.Exp, [cB3], [esB])
        dve_copy(es_hi[:], sink_sb[:], [esB], [esB])
        dve_tt(sink_sb[:], sink_sb[:], es_hi[:], ALU.subtract, [esB], [esB])
        dve_copy(es_lo[:], sink_sb[:], [esB], [esB])
        CONSTS = [constB, cB2, cB3, esB]

        x1B = [S.buf(f"x1_{i}", dma=True) for i in range(NBLK)]
        x2B = [S.buf(f"x2_{i}", dma=True) for i in range(NOWN)]
        outB = [S.buf(f"out_{i}", dma=True) for i in range(NOWN)]
        try:
            act(sc_sb[:], cvec_sb[:], AF.Silu, [cB3], [scB])
            dve_copy(sc_bf[:], sc_sb[:], [scB], [scB])
            w_ada_v = w_ada.rearrange("(kc p) n -> p kc n", p=128)
            vec_started = set()

            def mod_quarter(j, q4, streams):
                kind = j % 3
                sub = j // 3
                col0 = j * D + q4 * 512
                s0, s0B = load_w(w_ada_v[:, :, col0:col0 + 256])
                s1, s1B = load_w(w_ada_v[:, :, col0 + 256:col0 + 512])
                c0 = q4 * 4
                for si, (sn, off) in enumerate(streams):
                    bank_r = 2 + 2 * si
                    bank_c = 3 + 2 * si
                    mm(PS[bank_r][0:1, 0:256], [(sc_bf[:, off + kc:off + kc + 1], s0[:, kc, :]) for kc in range(16)],
                       [scB, s0B], writes=[PB[bank_r]])
                    mm(PS[bank_r][0:1, 256:512], [(sc_bf[:, off + kc:off + kc + 1], s1[:, kc, :]) for kc in range(16)],
                       [scB, s1B], pwrites=[PB[bank_r]])
                    row, rowB = next_T()
                    act(row[0:1, :], PS[bank_r][0:1, :], AF.Copy, [PB[bank_r]], [rowB])
                    for q in range(4):
                        mm(PS[bank_c][:, q:q + 1], [(row[0:1, q * 128:(q + 1) * 128], one11[0:1, 0:1])], [rowB, constB],
                           writes=[PB[bank_c]] if q == 0 else (), pwrites=() if q == 0 else [PB[bank_c]])
                    nm = ("B", "A", "G")[kind] + str(sub) + sn
                    vi = VIDX[nm]
                    first = nm not in vec_started
                    vec_started.add(nm)
                    wr = [vecB[vi]] if first else []
                    pw = [] if first else [vecB[vi]]
                    dst = vecs[:, vi, c0:c0 + 4]
                    bq_ = bpp[:, j * 16 + c0:j * 16 + c0 + 4]
                    pc = PS[bank_c][:, 0:4]
                    if kind == 0:
                        S.op("dve", (lambda e, dst=dst, pc=pc, bq_=bq_: e.tensor_tensor(out=dst, in0=pc, in1=bq_, op=ALU.add)),
                             reads=[PB[bank_c], cB3], writes=wr, pwrites=pw)
                    else:
                        gq_ = gpp[:, (2 * sub + (kind - 1)) * 16 + c0:(2 * sub + (kind - 1)) * 16 + c0 + 4]
                        tq, tqB = next_T()
                        if kind == 1:
                            S.op("dve", (lambda e, tq=tq, pc=pc, bq_=bq_: e.scalar_tensor_tensor(
                                out=tq[:, 0:4], in0=pc, scalar=1.0, in1=bq_, op0=ALU.add, op1=ALU.add)),
                                reads=[PB[bank_c], cB3], writes=[tqB])
                            S.op("dve", (lambda e, dst=dst, tq=tq, gq_=gq_: e.tensor_tensor(out=dst, in0=tq[:, 0:4], in1=gq_, op=ALU.mult)),
                                 reads=[tqB, cB3], writes=wr, pwrites=pw)
                        else:
                            coef = 1.0 if sub == 1 else 0.5
                            S.op("dve", (lambda e, tq=tq, pc=pc, bq_=bq_: e.tensor_tensor(out=tq[:, 0:4], in0=pc, in1=bq_, op=ALU.add)),
                                 reads=[PB[bank_c], cB3], writes=[tqB])
                            S.op("dve", (lambda e, dst=dst, tq=tq, gq_=gq_, coef=coef: e.scalar_tensor_tensor(
                                out=dst, in0=tq[:, 0:4], scalar=coef, in1=gq_, op0=ALU.mult, op1=ALU.mult)),
                                reads=[tqB, cB3], writes=wr, pwrites=pw)

            for j in range(2):
                for q4 in range(4):
                    mod_quarter(j, q4, [("x", 0), ("c", 16)])
            ffn_extra = [(lambda j=j, q4=q4: mod_quarter(j, q4, [("x", 0), ("c", 16)])) for j in range(2, 5) for q4 in range(4)]
            deferred_mod = [(lambda j=j, q4=q4: mod_quarter(j, q4, [("x", 0)])) for j in range(5, 9) for q4 in range(4)]

            _ck('mod')
            pass

            def expand_vec(vname, gi=0):
                vi = VIDX[vname]
                for q4 in range(4):
                    bank = 4 + q4
                    for c4 in range(4):
                        c = q4 * 4 + c4
                        tt, ttB = next_T()
                        dve_ts(tt[:, 0:128], ident_f[:], vecs[:, vi, c:c + 1], None, ALU.mult, None, [constB, vecB[vi]], [ttB])
                        if c4 == 0:
                            mm(PS[bank][:, c4 * 128:(c4 + 1) * 128], [(ones_f[:], tt[:, 0:128])], [ttB, constB], writes=[PB[bank]])
                        else:
                            mm(PS[bank][:, c4 * 128:(c4 + 1) * 128], [(ones_f[:], tt[:, 0:128])], [ttB, constB], pwrites=[PB[bank]])
                    cols = slice(q4 * 512, (q4 + 1) * 512)
                    if q4 == 0:
                        act(Gt[gi][:, cols], PS[bank][:], AF.Copy, [PB[bank]], [GB[gi]])
                    else:
                        S.op("act", (lambda e, cols=cols, bank=bank: e.activation(out=Gt[gi][:, cols], in_=PS[bank][:], func=AF.Copy)),
                             reads=[PB[bank]], pwrites=[GB[gi]])

            pre_state = {}

            def pre_block(src_ap, srcB, bi, sfx, sub):
                pre_a(src_ap, srcB, bi)
                pre_b(bi, sfx, sub)

            def pre_seq(items):
                a = [(lambda it=it: pre_a(it[0], it[1], it[2])) for it in items]
                b = [(lambda it=it: pre_b(it[2], it[3], it[4])) for it in items]
                out = []
                n = len(items)
                for k in range(n + 1):
                    if k < n:
                        out.append(a[k])
                    if k >= 1:
                        out.append(b[k - 1])
                return out

            def pre_a(src_ap, srcB, bi):
                xt, xtB = next_xs()
                dma("sp", xt[:], src_ap, [srcB] if srcB is not None else [], [xtB], xtB)
                sm = small[:, bi * 4:bi * 4 + 4]
                smB = smallB[bi]
                for q4 in range(4):
                    jt, jtB = next_Tb()
                    S.op("act", (lambda e, jt=jt, q4=q4, xt=xt, sm=sm: e.activation(
                        out=jt[:], in_=xt[:, q4 * 512:(q4 + 1) * 512], func=AF.Square, accum_out=sm[:, q4:q4 + 1])),
                        reads=[xtB], writes=[jtB, smB] if q4 == 0 else [jtB], pwrites=[] if q4 == 0 else [smB])
                s3 = small3[:, bi:bi + 1]
                r3 = small3[:, 4 + bi:5 + bi]
                S.op("dve", (lambda e, s3=s3, sm=sm: e.tensor_reduce(out=s3, in_=sm, axis=AX.X, op=ALU.add)),
                     reads=[smB], writes=[sm3B[bi]])
                act(s3, s3, AF.Sqrt, [sm3B[bi], constB], [sm3B[bi]], scale=1.0 / D, bias=epsT[:])
                dve_recip(r3, s3, [sm3B[bi]], [sm3B[bi]])
                xn, xnB = next_xn()
                dve_ts(xn[:], xt[:], r3, None, ALU.mult, None, [xtB, sm3B[bi]], [xnB])
                pre_state[bi] = (xn, xnB)

            def pre_b(bi, sfx, sub):
                xn, xnB = pre_state[bi]
                vA = VIDX["A%d%s" % (sub, sfx)]
                vBi = VIDX["B%d%s" % (sub, sfx)]
                for hb in range(2):
                    bank = hb
                    pv = PS[bank][:].bitcast(BF16).rearrange("p (c t) -> p c t", c=8)
                    transposes([(pv[:, c, :], xn[:, (hb * 8 + c) * 128:(hb * 8 + c + 1) * 128]) for c in range(8)],
                               ident_bf[:], [xnB, constB], writes=[PB[bank]])
                    for c in range(8):
                        cc = hb * 8 + c
                        wr = [hTB[bi]] if (hb == 0 and c == 0) else []
                        pw = [] if (hb == 0 and c == 0) else [hTB[bi]]
                        S.op("act", (lambda e, pv=pv, c=c, cc=cc: e.activation(
                            out=hT[:, cc, bi * 128:(bi + 1) * 128], in_=pv[:, c, :], func=AF.Identity,
                            scale=vecs[:, vA, cc:cc + 1], bias=vecs[:, vBi, cc:cc + 1])),
                            reads=[PB[bank], vecB[vA], vecB[vBi]], writes=wr, pwrites=pw)

            def post_block(bi, nb_cols, src_ap, srcB, dst_ap, dstB, gi=0, banks=(2, 3, 4, 5)):
                xt, xtB = next_xs()
                dma("sp", xt[:], src_ap, [srcB] if srcB is not None else [], [xtB], xtB)
                sm = small[:, 16 + bi * 4:16 + bi * 4 + 4]
                smB = smallB[bi]
                jt, jtB = next_T()
                for q4 in range(4):
                    bank = banks[q4]
                    transposes([(PS[bank][:, c4 * 128:(c4 + 1) * 128], ystage[:, q4 * 4 + c4, bi * 128:(bi + 1) * 128]) for c4 in range(4)],
                               ident_f[:], [ysB[q4 * 4 + c4] for c4 in range(4)] + [constB], writes=[PB[bank]])
                    S.op("act", (lambda e, bank=bank, q4=q4, jt=jt, sm=sm: e.activation(out=jt[:], in_=PS[bank][:], func=AF.Square,
                                                                                     accum_out=sm[:, q4:q4 + 1])),
                         reads=[PB[bank]], writes=[jtB, smB] if q4 == 0 else [jtB], pwrites=[] if q4 == 0 else [smB])
                st2 = small2[:, bi:bi + 1]
                S.op("dve", lambda e: e.tensor_reduce(out=st2, in_=sm, axis=AX.X, op=ALU.add), reads=[smB], writes=[smB])
                act(st2, st2, AF.Sqrt, [smB, constB], [smB], scale=1.0 / D, bias=epsT[:])
                dve_recip(st2, st2, [smB], [smB])
                for q4 in range(4):
                    bank = banks[q4]
                    cols = slice(q4 * 512, (q4 + 1) * 512)
                    tt, ttB = next_T()
                    dve_stt(tt[:], PS[bank][:], st2, Gt[gi][:, cols], ALU.mult, ALU.mult, [PB[bank], smB, GB[gi]], [ttB])
                    if q4 == 0:
                        dve_tt(xt[:, cols], xt[:, cols], tt[:], ALU.add, [ttB, xtB], [xtB])
                    else:
                        S.op("dve", (lambda e, cols=cols, tt=tt: e.tensor_tensor(out=xt[:, cols], in0=xt[:, cols], in1=tt[:], op=ALU.add)),
                             reads=[ttB], pwrites=[xtB])
                dma("sp", dst_ap, xt[:], [xtB], [dstB], dstB)
                if DEBUG and dst_ap.tensor.name in ("x1", "x2"):
                    dd = x1d if dst_ap.tensor.name == "x1" else x2d
                    if not dbgB:
                        dbgB.append(S.buf("dbg", dma=True))
                    db = dbgB[0]
                    r0 = dst_ap.offset // D
                    S.op("sp", (lambda e, dd=dd, r0=r0, xt=xt: e.dma_start(out=dd[r0:r0 + 128, :], in_=xt[:])), reads=[xtB], pwrites=[db], dma=db)

            def ffn(l, sub, tiles, src_of, dst_of, tail_sec=None, skip_first_pre=False):
                wi = w_ffn_in[l].rearrange("(kc p) n -> p kc n", p=128)
                wo = w_ffn_out[l].rearrange("(f p) n -> p f n", p=128)
                cur_g = [None]

                def need_g(sfx):
                    nm = "G%d%s" % (sub, sfx)
                    if cur_g[0] != nm:
                        expand_vec(nm)
                        cur_g[0] = nm

                def pre_ops(blocks):
                    items = []
                    for bi, blk in enumerate(blocks):
                        sap, sB, sfx = src_of(blk)
                        items.append((sap, sB, bi, sfx, sub))
                    return pre_seq(items)

                def post_ops(blocks):
                    def mk(bi, blk):
                        def f():
                            sap, sB, sfx = src_of(blk)
                            dap, dB = dst_of(blk)
                            need_g(sfx)
                            post_block(bi, len(blocks) * 128, sap, sB, dap, dB, 0, banks=(0, 1, 6, 7))
                        return f
                    return [mk(bi, blk) for bi, blk in enumerate(blocks)]

                def A_ops(blocks):
                    nb = len(blocks)
                    tt_ = nb * 128
                    hreads = [hTB[bi] for bi in range(nb)]

                    def mk(fp):
                        def f():
                            sg_, sgB = load_w(wi[:, :, fp * 256:(fp + 1) * 256])
                            su_, suB = load_w(wi[:, :, DFF + fp * 256:DFF + (fp + 1) * 256])
                            for s2 in range(2):
                                ff = fp * 2 + s2
                                bg = 2 + (ff % 2) * 2
                                bu = bg + 1
                                mm(PS[bg][:, 0:tt_], [(sg_[:, kc, s2 * 128:(s2 + 1) * 128], hT[:, kc, 0:tt_]) for kc in range(16)],
                                   [sgB] + hreads, writes=[PB[bg]])
                                mm(PS[bu][:, 0:tt_], [(su_[:, kc, s2 * 128:(s2 + 1) * 128], hT[:, kc, 0:tt_]) for kc in range(16)],
                                   [suB] + hreads, writes=[PB[bu]])
                                tt, ttB = next_T()
                                act(tt[:, 0:tt_], PS[bg][:, 0:tt_], AF.Silu, [PB[bg]], [ttB])
                                dve_tt(h1[:, ff, 0:tt_], tt[:, 0:tt_], PS[bu][:, 0:tt_], ALU.mult, [ttB, PB[bu]], [h1B[ff]])
                        return f
                    return [mk(fp) for fp in range(22)]

                def B_ops(blocks):
                    nb = len(blocks)
                    tt_ = nb * 128

                    def mk(cp):
                        def f():
                            for j, (f0, nk) in enumerate(((0, 16), (16, 16), (32, 12))):
                                sl, slB = load_w(wo[:, f0:f0 + nk, cp * 256:(cp + 1) * 256], nk)
                                for s2 in range(2):
                                    by = 6 + s2
                                    pairs = [(sl[:, fi, s2 * 128:(s2 + 1) * 128], h1[:, f0 + fi, 0:tt_]) for fi in range(nk)]
                                    mm(PS[by][:, 0:tt_], pairs, [slB] + h1B[f0:f0 + nk],
                                       writes=[PB[by]] if j == 0 else (), pwrites=() if j == 0 else [PB[by]],
                                       start=(j == 0), stop=(j == 2))
                            for s2 in range(2):
                                c = cp * 2 + s2
                                act(ystage[:, c, 0:tt_], PS[6 + s2][:, 0:tt_], AF.Copy, [PB[6 + s2]], [ysB[c]])
                            if ffn_extra:
                                ffn_extra.pop(0)()
                        return f
                    return [mk(cp) for cp in range(8)]

                def interleave(prim, sec, frac=1.0):
                    n, m = len(prim), len(sec)
                    pos = {}
                    for k in range(m):
                        p = min(n - 1, int((k + 0.5) * n * frac / m))
                        pos.setdefault(p, []).append(sec[k])
                    for i, op_ in enumerate(prim):
                        op_()
                        for o in pos.get(i, []):
                            o()

                if not skip_first_pre:
                    for o in pre_ops(tiles[0]):
                        o()
                for t, blocks in enumerate(tiles):
                    if t == 0:
                        _ck('pre')
                    interleave(A_ops(blocks), post_ops(tiles[t - 1]) if t > 0 else [], 0.7)
                    if t == 0:
                        _ck('phaseA')
                    interleave(B_ops(blocks), pre_ops(tiles[t + 1]) if t + 1 < len(tiles) else (tail_sec or []), 1.0)
                    if t == 0:
                        _ck('phaseB')
                for o in post_ops(tiles[-1]):
                    o()


            def blkrows(ap, b):
                return ap[b * 128:(b + 1) * 128, :]

            tiles0 = [[0, 1, 2, 3], [4, 5, 6, 7], [8, 9, 10, 11], [12, 13, 14, 15], [16, 17, 18]]
            kvA = [4, 8, 12]
            ffn(0, 0, tiles0,
                lambda b: (blkrows(xe, b), None, "c" if b >= CTX0 else "x"),
                lambda b: (blkrows(x1, b), x1B[b]),
                tail_sec=pre_seq([(blkrows(x1, blk), x1B[blk], bi, "x", 1) for bi, blk in enumerate(kvA)]))
            _ck('ffn0')
            pass

            w_in_v = w_in.rearrange("(kc p) n -> p kc n", p=128)

            def kv_compute(blocks, wk_, wkB, wv_, wvB, skip=()):
                nb = len(blocks)
                tt_ = nb * 128
                hreads = [hTB[bi] for bi in range(nb)]
                for g in range(2):
                    bk = 2 + g
                    mm(PS[bk][:, 0:tt_], [(wk_[:, kc, g * 128:(g + 1) * 128], hT[:, kc, 0:tt_]) for kc in range(16)],
                       [wkB] + hreads, writes=[PB[bk]])
                    kb_, kbB = next_Tb()
                    dve_copy(kb_[:, 0:tt_], PS[bk][:, 0:tt_], [PB[bk]], [kbB])
                    br = 4 + g
                    mm(PS[br][:, 0:tt_], [(rmat[:], kb_[:, 0:tt_])], [kbB, cB2], writes=[PB[br]])
                    for bi, blk in enumerate(blocks):
                        if blk in skip:
                            continue
                        cs = slice(bi * 128, (bi + 1) * 128)
                        dst = kT_all[:, g, blk * 128:(blk + 1) * 128]
                        wr = [kTB[blk]] if g == 0 else []
                        pw = [] if g == 0 else [kTB[blk]]
                        if blk < CTX0:
                            t1, t1B = next_T()
                            t2, t2B = next_T()
                            dve_tt(t1[:, 0:128], PS[bk][:, cs], cos_sb[:, cs], ALU.mult, [PB[bk], csB], [t1B])
                            dve_tt(t2[:, 0:128], PS[br][:, cs], sin_sb[:, cs], ALU.mult, [PB[br], csB], [t2B])
                            S.op("dve", (lambda e, dst=dst, t1=t1, t2=t2: e.tensor_tensor(out=dst, in0=t1[:, 0:128], in1=t2[:, 0:128], op=ALU.add)),
                                 reads=[t1B, t2B], writes=wr, pwrites=pw)
                        else:
                            S.op("dve", (lambda e, dst=dst, bk=bk, cs=cs: e.tensor_copy(out=dst, in_=PS[bk][:, cs])),
                                 reads=[PB[bk]], writes=wr, pwrites=pw)
                for bi, blk in enumerate(blocks):
                    if blk in skip:
                        continue
                    bv = 6 + (bi % 2)
                    mm(PS[bv][:, 0:256], [(hT[:, kc, bi * 128:(bi + 1) * 128], wv_[:, kc, :]) for kc in range(16)],
                       [wvB, hTB[bi]], writes=[PB[bv]])
                    act(v_all[:, blk, :], PS[bv][:, 0:256], AF.Copy, [PB[bv]], [vB[blk]])

            wk0, wk0B = load_w(w_in_v[:, :, 1024:1280])
            wv0, wv0B = load_w(w_in_v[:, :, 1280:1536])
            pre_done = set()
            for bi, blk in enumerate(kvA):
                S.op("sp", (lambda e, bi=bi, blk=blk: e.dma_start(out=cos_sb[:, bi * 128:(bi + 1) * 128], in_=cosT[:, blk * 128:(blk + 1) * 128])),
                     writes=[csB] if bi == 0 else (), pwrites=() if bi == 0 else [csB], dma=csB)
                S.op("sp", (lambda e, bi=bi, blk=blk: e.dma_start(out=sin_sb[:, bi * 128:(bi + 1) * 128], in_=sinT[:, blk * 128:(blk + 1) * 128])),
                     pwrites=[csB], dma=csB)
            kv_compute(kvA, wk0, wk0B, wv0, wv0B)
            pre_done.update(kvA)
            for blocks in ([16, 17, 18],):
                nb = len(blocks)
                for o_ in pre_seq([(blkrows(x1, blk), x1B[blk], bi, "c" if blk >= CTX0 else "x", 1) for bi, blk in enumerate(blocks)]):
                    o_()
                t0 = blocks[0] * 128
                nrope = sum(1 for b in blocks if b < CTX0)
                dma("sp", cos_sb[:, 0:nrope * 128], cosT[:, t0:t0 + nrope * 128], [], [csB], csB)
                S.op("sp", (lambda e, t0=t0, nrope=nrope: e.dma_start(out=sin_sb[:, 0:nrope * 128], in_=sinT[:, t0:t0 + nrope * 128])),
                     pwrites=[csB], dma=csB)
                kv_compute(blocks, wk0, wk0B, wv0, wv0B)
                pre_done.update(blocks)

            _ck('kv')
            tilesM = [[0, 1, 2, 3], [4, 5, 6, 7], [8, 9, 10, 11], [12, 13, 14, 15]]
            lng = lngb_sb[:, 0:8]
            lnb = lngb_sb[:, 8:16]
            pend_post = []

            def run_post(k=1):
                for _ in range(k):
                    if pend_post:
                        pend_post.pop(0)()

            for ti, blocks in enumerate(tilesM):
                nb = 4
                tt_ = 512
                t0 = blocks[0] * 128
                if ti == 0:
                    for o_ in pre_seq([(blkrows(x1, blk), x1B[blk], bi, "x", 1) for bi, blk in enumerate(blocks)]):
                        o_()
                if ti == 1:
                    expand_vec("G1x")
                hreads = [hTB[bi] for bi in range(4)]
                dma("sp", cos_sb[:], cosT[:, t0:t0 + 512], [], [csB], csB)
                S.op("sp", (lambda e, t0=t0: e.dma_start(out=sin_sb[:], in_=sinT[:, t0:t0 + 512])), pwrites=[csB], dma=csB)

                def proj_chunk(sl, slB, s2, bank):
                    mm(PS[bank][:], [(sl[:, kc, s2 * 128:(s2 + 1) * 128], hT[:, kc, :]) for kc in range(16)],
                       [slB] + hreads, writes=[PB[bank]])

                pend_rot = []

                def q_finish(h, bq, br, qb_, qbB, t1, t1B):
                    mm(PS[br][:], [(rmat[:], qb_[:])], [qbB, cB2], writes=[PB[br]])
                    t2, t2B = next_T()
                    dve_tt(t2[:], PS[br][:], sin_sb[:], ALU.mult, [PB[br], csB], [t2B])
                    dve_tt(qT[:, h, :], t1[:], t2[:], ALU.add, [t1B, t2B], [qTB[h]])

                for cp in range(4):
                    sl, slB = load_w(w_in_v[:, :, cp * 256:(cp + 1) * 256])
                    for s2 in range(2):
                        h = cp * 2 + s2
                        bq = 2 + (h % 2)
                        br = 4 + (h % 2)
                        proj_chunk(sl, slB, s2, bq)
                        qb_, qbB = next_Tb()
                        dve_copy(qb_[:], PS[bq][:], [PB[bq]], [qbB])
                        t1, t1B = next_T()
                        dve_tt(t1[:], PS[bq][:], cos_sb[:], ALU.mult, [PB[bq], csB], [t1B])
                        if pend_rot:
                            pend_rot.pop(0)()
                        pend_rot.append(lambda h=h, bq=bq, br=br, qb_=qb_, qbB=qbB, t1=t1, t1B=t1B: q_finish(h, bq, br, qb_, qbB, t1, t1B))
                    while pend_rot:
                        pend_rot.pop(0)()
                    run_post(1)
                wkt, wktB = load_w(w_in_v[:, :, 1024:1280])
                wvt, wvtB = load_w(w_in_v[:, :, 1280:1536])
                kv_compute(blocks, wkt, wktB, wvt, wvtB, skip=pre_done)
                run_post(4)
                _ck('mq')
                vgf = ystage
                pend_stats = []
                for cp in range(4):
                    sl, slB = load_w(w_in_v[:, :, 2560 + cp * 256:2560 + (cp + 1) * 256])
                    for s2 in range(2):
                        c = cp * 2 + s2
                        bq = 2 + (c % 2)
                        proj_chunk(sl, slB, s2, bq)
                        act(vgf[:, c, :], PS[bq][:], AF.Gelu_apprx_tanh, [PB[bq]], [ysB[c]])
                        sq, sqB = next_T()
                        act(sq[:], vgf[:, c, :], AF.Square, [ysB[c]], [sqB])
                        if pend_stats:
                            pend_stats.pop(0)()

                        def stats(c=c, sq=sq, sqB=sqB):
                            mm(PS[6][:], [(ones_f[:], vgf[:, c, :])], [ysB[c], constB], writes=[PB[6]] if c == 0 else (),
                               pwrites=() if c == 0 else [PB[6]], start=(c == 0), stop=(c == 7))
                            mm(PS[7][:], [(ones_f[:], sq[:])], [sqB, constB], writes=[PB[7]] if c == 0 else (),
                               pwrites=() if c == 0 else [PB[7]], start=(c == 0), stop=(c == 7))
                        pend_stats.append(stats)
                while pend_stats:
                    pend_stats.pop(0)()
                mu = cos_sb
                var = sin_sb
                act(mu[:], PS[6][:], AF.Identity, [PB[6]], [csB], scale=1.0 / 1024)
                msq, msqB = next_T()
                dve_tt(msq[:], mu[:], mu[:], ALU.mult, [csB], [msqB])
                S.op("dve", lambda e, msq=msq, var=var: e.scalar_tensor_tensor(out=var[:], in0=PS[7][:], scalar=1.0 / 1024, in1=msq[:],
                                                             op0=ALU.mult, op1=ALU.subtract),
                     reads=[PB[7], msqB], pwrites=[csB])
                S.op("act", lambda e, var=var: e.activation(out=var[:], in_=var[:], func=AF.Sqrt, bias=epsT[:]),
                     reads=[csB, constB], pwrites=[csB])
                S.op("dve", lambda e, var=var: e.reciprocal(out=var[:], in_=var[:]), reads=[csB], pwrites=[csB])
                _ck('mvg')

                u_state = {}

                def u_chunk(c):
                    if c % 2 == 0:
                        u_state["sl"] = load_w(w_in_v[:, :, 1536 + (c // 2) * 256:1536 + (c // 2 + 1) * 256])
                    sl, slB = u_state["sl"]
                    bq = c % 2
                    proj_chunk(sl, slB, c % 2, bq)
                    act(uT[:, c, :], PS[bq][:], AF.Gelu_apprx_tanh, [PB[bq]], [uTB[c]])

                def norm_pair(c0):
                    for c in (c0, c0 + 1):
                        tt, ttB = next_T()
                        dve_tt(tt[:], vgf[:, c, :], mu[:], ALU.subtract, [ysB[c], csB], [ttB])
                        dve_tt(tt[:], tt[:], var[:], ALU.mult, [ttB, csB], [ttB])
                        act(vnT[:, c, :], tt[:], AF.Identity, [ttB, cB3], [vnTB[c]], scale=lng[:, c:c + 1], bias=lnb[:, c:c + 1])

                def att_unit(bi, qb, g, mid=None):
                    prev = (qb - 1, 0) if qb > 0 else (HALO, 2)
                    nxt = (qb + 1, 1) if qb < NOWN - 1 else (HALO, 3)
                    kbs = [(prev[0], prev[1]), (qb, None), (nxt[0], nxt[1]), (CTX0, None), (CTX0 + 1, None)]
                    qv = qT[:, 4 * g:4 * g + 4, bi * 128:(bi + 1) * 128]
                    qreads = qTB[4 * g:4 * g + 4]
                    bO, bD = 5, 6
                    sbanks = [2, 3, 4]

                    def s_mm(i):
                        kb, mi = kbs[i]
                        bank = sbanks[i % 3]
                        pairs = [(kT_all[:, g, kb * 128:(kb + 1) * 128], qv)]
                        if mi is not None:
                            pairs.append((ident_bf[:], mask_sb[:, mi, :].unsqueeze(1).to_broadcast([128, 4, 128])))
                        mm(PS[bank][:].rearrange("p (h t) -> p h t", h=4), pairs,
                           [kTB[kb], cB2, constB] + qreads, writes=[PB[bank]])
                    s_mm(0)
                    s_mm(1)
                    s_mm(2)
                    if mid is not None:
                        mid()
                    for i in range(5):
                        kb, mi = kbs[i]
                        bank = sbanks[i % 3]
                        pT, pTB = next_Tb()
                        act(pT[:], PS[bank][:], AF.Exp, [PB[bank]], [pTB], scale=SCALE)
                        mm(PS[bO][:], [(v_all[:, kb, g * 128:(g + 1) * 128], pT[:])], [vB[kb], pTB],
                           writes=[PB[bO]] if i == 0 else (), pwrites=() if i == 0 else [PB[bO]],
                           start=(i == 0), stop=(i == 4))
                        pairsD = [(ones_bf[:], pT[:].rearrange("p (h t) -> p h t", h=4))]
                        if i == 4:
                            pairsD.append((ones_bf[0:1, :], es_hi[0:1, 4 * g:4 * g + 4].unsqueeze(2).to_broadcast([1, 4, 128])))
                            pairsD.append((ones_bf[0:1, :], es_lo[0:1, 4 * g:4 * g + 4].unsqueeze(2).to_broadcast([1, 4, 128])))
                        mm(PS[bD][:].rearrange("p (h t) -> p h t", h=4), pairsD, [pTB, constB, esB],
                           writes=[PB[bD]] if i == 0 else (), pwrites=() if i == 0 else [PB[bD]],
                           start=(i == 0), stop=(i == 4))
                        if i + 3 < 5:
                            s_mm(i + 3)
                    rD, rDB = next_T()
                    dve_recip(rD[:], PS[bD][:], [PB[bD]], [rDB])
                    ov = mbT[:, 4 * g:4 * g + 4, bi * 128:(bi + 1) * 128]
                    S.op("dve", (lambda e, ov=ov, rD=rD, bO=bO: e.tensor_tensor(
                        out=ov, in0=PS[bO][:].rearrange("p (h t) -> p h t", h=4),
                        in1=rD[:].rearrange("p (h t) -> p h t", h=4), op=ALU.mult)),
                        reads=[PB[bO], rDB], writes=[mbTB[bi][g]])

                units = [(bi, qb, g) for bi, qb in enumerate(blocks) for g in range(2)]
                for ui, (bi, qb, g) in enumerate(units):
                    att_unit(bi, qb, g, mid=(lambda ui=ui: u_chunk(ui)))
                    if ui % 2 == 0:
                        norm_pair(ui)
                _ck('matt')
                for bi in range(4):
                    bank = bi % 2
                    pv = PS[bank][:].bitcast(BF16)
                    transposes([(pv[:, c * 128:(c + 1) * 128], vnT[:, c, bi * 128:(bi + 1) * 128]) for c in range(8)],
                               ident_bf[:], vnTB + [constB], writes=[PB[bank]])
                    dve_copy(vn_tm[:, bi, :], pv[:, :], [PB[bank]], [vntmB[bi]])
                for bi in range(4):
                    for hb in range(2):
                        bank = 4 + hb
                        for g4 in range(4):
                            g = hb * 4 + g4
                            o = PS[bank][:, g4 * 128:(g4 + 1) * 128]
                            pairs = [(vn_tm[:, bi, g * 128:(g + 1) * 128], wsT_sb[:, g, :]),
                                     (ones_bf[0:1, :], bs_sb[0:1, g * 128:(g + 1) * 128])]
                            if g4 == 0:
                                mm(o, pairs, [vntmB[bi], cB2, constB], writes=[PB[bank]])
                            else:
                                mm(o, pairs, [vntmB[bi], cB2, constB], pwrites=[PB[bank]])
                        uv = uT[:, hb * 4:hb * 4 + 4, bi * 128:(bi + 1) * 128]
                        S.op("dve", (lambda e, uv=uv, bank=bank: e.tensor_tensor(
                            out=uv, in0=uv, in1=PS[bank][:].rearrange("p (g t) -> p g t", g=4), op=ALU.mult)),
                            reads=[PB[bank]] + uTB[hb * 4:hb * 4 + 4], pwrites=uTB[hb * 4:hb * 4 + 4])
                mbreads = [mbTB[bi][g] for bi in range(4) for g in range(2)]
                _ck('msp')
                if blocks[0] == 0:
                    dump("d_q", qT, qTB)
                    dump("d_k", kT_all, kTB)
                    dump("d_v", v_all, vB)
                    dump("d_h", hT[:], hreads)
                wa_v = w_bra.rearrange("(kc p) n -> p kc n", p=128)
                wb_v = w_brb.rearrange("(kc p) n -> p kc n", p=128)
                for cp in range(8):
                    sga, sgaB = load_w(w_in_v[:, :, 3584 + cp * 256:3584 + (cp + 1) * 256])
                    sgb, sgbB = load_w(w_in_v[:, :, 5632 + cp * 256:5632 + (cp + 1) * 256])
                    for s2 in range(2):
                        c = cp * 2 + s2
                        if s2 == 0:
                            swa, swaB = load_w(wa_v[:, :, cp * 256:(cp + 1) * 256], 8)
                            swb, swbB = load_w(wb_v[:, :, cp * 256:(cp + 1) * 256], 8)
                        b_ga, b_gb, b_pa, b_pb = (2, 3, 4, 5) if c % 2 == 0 else (6, 7, 0, 1)
                        proj_chunk(sga, sgaB, s2, b_ga)
                        proj_chunk(sgb, sgbB, s2, b_gb)
                        mm(PS[b_pa][:], [(swa[:, kc, s2 * 128:(s2 + 1) * 128], uT[:, kc, :]) for kc in range(8)],
                           [swaB] + uTB, writes=[PB[b_pa]])
                        mm(PS[b_pb][:], [(swb[:, kc, s2 * 128:(s2 + 1) * 128], mbT[:, kc, :]) for kc in range(8)],
                           [swbB] + mbreads, writes=[PB[b_pb]])
                        ta, taB = next_T()
                        tb, tbB = next_T()
                        act(ta[:], PS[b_ga][:], AF.Sigmoid, [PB[b_ga]], [taB])
                        act(tb[:], PS[b_gb][:], AF.Sigmoid, [PB[b_gb]], [tbB])
                        dve_tt(ta[:], ta[:], PS[b_pa][:], ALU.mult, [taB, PB[b_pa]], [taB])
                        dve_tt(tb[:], tb[:], PS[b_pb][:], ALU.mult, [tbB, PB[b_pb]], [tbB])
                        dve_tt(mgT[:, c, :], ta[:], tb[:], ALU.add, [taB, tbB], [mgTB[c]] + vnTB + vntmB)
                if blocks[0] == 0:
                    dump("d_ma", uT, uTB)
                    dump("d_mb", mbT, mbreads)
                    dump("d_mg", mgT, mgTB)
                _ck('mmerge')
                nxt_blocks = tilesM[ti + 1] if ti + 1 < len(tilesM) else []
                if nxt_blocks:
                    nxt_pre = pre_seq([(blkrows(x1, b_), x1B[b_], nbi, "x", 1) for nbi, b_ in enumerate(nxt_blocks)])
                else:
                    nxt_pre = pre_seq([(blkrows(x2, b_), x2B[b_], nbi, "x", 2) for nbi, b_ in enumerate(tilesM[0])])
                for cp in range(8):
                    sl, slB = load_w(w_o.rearrange("(kc p) n -> p kc n", p=128)[:, :, cp * 256:(cp + 1) * 256])
                    for s2 in range(2):
                        c = cp * 2 + s2
                        by = 6 + (c % 2)
                        mm(PS[by][:], [(sl[:, kc, s2 * 128:(s2 + 1) * 128], mgT[:, kc, :]) for kc in range(16)],
                           [slB] + mgTB, writes=[PB[by]])
                        act(ystage[:, c, :], PS[by][:], AF.Copy, [PB[by]], [ysB[c]])
                    if nxt_pre:
                        nxt_pre.pop(0)()
                    if cp % 2 == 0 and deferred_mod:
                        deferred_mod.pop(0)()
                while nxt_pre:
                    nxt_pre.pop(0)()
                for bi, blk in enumerate(blocks):
                    pend_post.append(lambda bi=bi, blk=blk: post_block(bi, 512, blkrows(x1, blk), x1B[blk], blkrows(x2, blk), x2B[blk],
                                                                        0, banks=(0, 1, 6, 7)))
                _ck('mtile0')
            while deferred_mod:
                deferred_mod.pop(0)()
            run_post(8)
            _ck('mixer')
            pass

            ffn(1, 2, tilesM,
                lambda b: (blkrows(x2, b), x2B[b], "x"),
                lambda b: (blkrows(outp, b), outB[b]), skip_first_pre=True)


        except _Stop:
            pass
        S.wait_all("sp", outB + x1B + x2B + dbgB)
        S.emit()
    return nc


def _rope_tables(positions):
    pos = np.asarray(positions, dtype=np.float32)
    row = np.floor(pos / 64.0).astype(np.float32)
    col = (pos - row * 64.0).astype(np.float32)
    freqs = (np.float32(10000.0) ** (-np.arange(32, dtype=np.float32) / np.float32(32))).astype(np.float32)
    ang_r = row[:, None] * freqs[None, :]
    ang_c = col[:, None] * freqs[None, :]
    ang = np.concatenate([ang_r, ang_r, ang_c, ang_c], axis=-1).astype(np.float32)
    return np.ascontiguousarray(np.cos(ang).T.astype(np.float32)), np.ascontiguousarray(np.sin(ang).T.astype(np.float32))


def _const_tables(half):
    RT = np.zeros((128, 128), np.float32)
    for m in range(128):
        if (m % 64) < 32:
            RT[m + 32, m] = -1.0
        else:
            RT[m - 32, m] = 1.0
    NEG = -30000.0
    kj = np.arange(128)[:, None]
    qi = np.arange(128)[None, :]
    tri_prev = np.where(kj >= qi, 0.0, NEG).astype(np.float32)
    tri_next = np.where(kj <= qi, 0.0, NEG).astype(np.float32)
    allm = np.full((128, 128), NEG, np.float32)
    m_first = allm if half == 0 else tri_prev
    m_last = tri_next if half == 0 else allm
    ms = np.stack([m for m in (tri_prev, tri_next, m_first, m_last)], axis=1)
    return RT, np.ascontiguousarray(ms.astype(np.float32))


_NC_CACHE = {}


def prep_inputs(x, c, ctx, c_ctx, w_ada, b_ada, norm_g, w_ffn_in, w_ffn_out, w_in,
                gmlp_ln_g, gmlp_ln_b, w_spatial, b_spatial, attn_sink,
                w_branch_a, w_branch_b, w_out, cores=range(8)):
    f = lambda a: np.ascontiguousarray(np.asarray(a, dtype=np.float32))
    x, c, ctx, c_ctx = f(x), f(c), f(ctx), f(c_ctx)
    shared = {
        "w_ada": f(w_ada)[0],
        "bpp": np.ascontiguousarray(f(b_ada)[0].reshape(144, 128).T),
        "gpp": np.ascontiguousarray(f(norm_g)[0].reshape(96, 128).T),
        "w_ffn_in": f(w_ffn_in)[0], "w_ffn_out": f(w_ffn_out)[0], "w_in": f(w_in)[0],
        "lngb": np.ascontiguousarray(np.concatenate([f(gmlp_ln_g)[0].reshape(8, 128).T, f(gmlp_ln_b)[0].reshape(8, 128).T], axis=1)),
        "w_sT": np.ascontiguousarray(f(w_spatial)[0].transpose(2, 0, 1)),
        "b_sp": f(b_spatial)[0].reshape(1, 1024),
        "sink": f(attn_sink)[0].reshape(1, 8),
        "w_bra": f(w_branch_a)[0], "w_brb": f(w_branch_b)[0], "w_o": f(w_out)[0],
    }
    in_maps = []
    for core in cores:
        b, half = core // 2, core % 2
        own = np.arange(half * 2048, (half + 1) * 2048)
        halo = np.arange(2048, 2176) if half == 0 else np.arange(1920, 2048)
        xe = np.concatenate([x[b, own], x[b, halo], ctx[b]], axis=0)
        cosT, sinT = _rope_tables(np.concatenate([own, halo]))
        RT, ms = _const_tables(half)
        cvec = np.concatenate([c[b].reshape(16, 128).T, c_ctx.reshape(16, 128).T], axis=1)
        m = dict(shared)
        m.update({"xe": np.ascontiguousarray(xe), "cvec": np.ascontiguousarray(cvec), "cosT": cosT, "sinT": sinT,
                  "rmatT": RT, "masks": ms})
        in_maps.append(m)
    return in_maps


def kernel(**inputs):
    in_maps = prep_inputs(**inputs)
    if "nc" not in _NC_CACHE:
        _NC_CACHE["nc"] = build_program()
    res = run_bass_kernel_spmd(_NC_CACHE["nc"], in_maps, core_ids=list(range(8)))
    out = np.empty((4, 4096, 2048), np.float32)
    for core in range(8):
        b, half = core // 2, core % 2
        out[b, half * 2048:(half + 1) * 2048] = res.results[core]["out"]
    return out
```

```python
import contextlib
import numpy as np
import concourse.bass as bass
import concourse.mybir as mybir
from concourse.bass_utils import run_bass_kernel_spmd

F32 = mybir.dt.float32
BF16 = mybir.dt.bfloat16
AF = mybir.ActivationFunctionType
ALU = mybir.AluOpType
AX = mybir.AxisListType

D = 2048
DFF = 5632
NOWN = 16
NBLK = 19
HALO = 16
CTX0 = 17
EPS = 1e-6
SCALE = 128.0 ** -0.5
NSLOT = 6
STOP = None
DEBUG = False


class _Stop(Exception):
    pass


def _ck(name):
    if STOP == name:
        raise _Stop()


class Buf:
    __slots__ = ("name", "w", "r", "prev_r", "dsem", "dval")

    def __init__(self, name):
        self.name = name
        self.w = {}
        self.r = {}
        self.prev_r = {}
        self.dsem = None
        self.dval = 0


def _merge(dst, src):
    for k, v in src.items():
        if dst.get(k, 0) < v:
            dst[k] = v


class Stream:
    def __init__(self, name, sem_id):
        self.name = name
        self.sem_id = sem_id
        self.count = 0
        self.ops = []
        self.seen = {}


class Sched:
    ENGS = ("pe", "act", "dve", "pool", "sp")

    def __init__(self, nc, stack):
        self.nc = nc
        self.stack = stack
        self.sems = []
        self.streams = {}
        for n in self.ENGS:
            self.streams[n] = Stream(n, self.new_sem("c_" + n))
        self.nbuf = 0

    def new_sem(self, name):
        h = self.stack.enter_context(self.nc.semaphore(name))
        self.sems.append(h)
        return len(self.sems) - 1

    def buf(self, name=None, dma=False):
        self.nbuf += 1
        b = Buf(name or f"b{self.nbuf}")
        if dma:
            b.dsem = self.new_sem("d%d" % self.nbuf)
        return b

    def op(self, eng, fn, reads=(), writes=(), pwrites=(), dma=None):
        st = self.streams[eng]
        raw = {}
        other = {}
        for b in reads:
            _merge(raw, b.w)
        for b in writes:
            _merge(other, b.r)
            _merge(other, b.w)
        for b in pwrites:
            _merge(other, b.prev_r)
        _merge(raw, other)
        waits = []
        for k, v in raw.items():
            if st.seen.get(k, 0) < v:
                st.seen[k] = v
                waits.append((k, v))
        if dma is not None:
            dma.dval += 16
            tok = (dma.dsem, dma.dval)
            inc = (dma.dsem, 16)
        else:
            st.count += 1
            tok = (st.sem_id, st.count)
            inc = (st.sem_id, 1)
        st.ops.append((waits, fn, inc))
        for b in reads:
            if b.r.get(tok[0], 0) < tok[1]:
                b.r[tok[0]] = tok[1]
        for b in writes:
            b.prev_r = b.r
            b.r = {}
            b.w = {tok[0]: tok[1]}
        for b in pwrites:
            if b.w.get(tok[0], 0) < tok[1]:
                b.w[tok[0]] = tok[1]
        return tok

    def wait_all(self, eng, bufs):
        st = self.streams[eng]
        deps = {}
        for b in bufs:
            _merge(deps, b.r)
            _merge(deps, b.w)
        waits = []
        for k, v in deps.items():
            if st.seen.get(k, 0) < v:
                st.seen[k] = v
                waits.append((k, v))
        st.ops.append((waits, None, None))

    def barrier(self):
        cur = {self.streams[n].sem_id: self.streams[n].count for n in ("pe", "act", "dve", "pool")}
        for n in self.ENGS:
            st = self.streams[n]
            waits = []
            for k, v in cur.items():
                if v > 0 and st.seen.get(k, 0) < v:
                    st.seen[k] = v
                    waits.append((k, v))
            st.ops.append((waits, None, None))

    def emit(self):
        nc = self.nc
        sems = self.sems

        def replay(st):
            def run(e):
                for waits, fn, inc in st.ops:
                    for k, v in waits:
                        e.wait_ge(sems[k], v)
                    if fn is not None:
                        ins = fn(e)
                        ins.then_inc(sems[inc[0]], inc[1])
            return run

        with nc.Block() as block:
            block.tensor(replay(self.streams["pe"]))
            block.scalar(replay(self.streams["act"]))
            block.vector(replay(self.streams["dve"]))
            block.gpsimd(replay(self.streams["pool"]))
            block.sync(replay(self.streams["sp"]))


def build_program():
    nc = bass.Bass("TRN2", target_bir_lowering=False)

    def din(name, shape):
        return nc.dram_tensor(name, shape, F32, kind="ExternalInput").ap()

    xe = din("xe", [NBLK * 128, D])
    cvec = din("cvec", [128, 32])
    w_ada = din("w_ada", [D, 9 * D])
    bpp_d = din("bpp", [128, 144])
    gpp_d = din("gpp", [128, 96])
    w_ffn_in = din("w_ffn_in", [2, D, 2 * DFF])
    w_ffn_out = din("w_ffn_out", [2, DFF, D])
    w_in = din("w_in", [D, 7680])
    lngb = din("lngb", [128, 16])
    w_sT = din("w_sT", [128, 8, 128])
    b_sp = din("b_sp", [1, 1024])
    sink = din("sink", [1, 8])
    w_bra = din("w_bra", [1024, D])
    w_brb = din("w_brb", [1024, D])
    w_o = din("w_o", [D, D])
    cosT = din("cosT", [128, 17 * 128])
    sinT = din("sinT", [128, 17 * 128])
    rmatT = din("rmatT", [128, 128])
    masks = din("masks", [128, 4, 128])
    x1 = nc.dram_tensor("x1", [NBLK * 128, D], F32).ap()
    x2 = nc.dram_tensor("x2", [NOWN * 128, D], F32).ap()
    x1d = x2d = None
    if DEBUG:
        x1d = nc.dram_tensor("x1d", [NBLK * 128, D], F32, kind="ExternalOutput").ap()
        x2d = nc.dram_tensor("x2d", [NOWN * 128, D], F32, kind="ExternalOutput").ap()
    dbgB = []
    dd = {}
    if DEBUG:
        for nm, shp in (("d_q", [128, 8, 512]), ("d_k", [128, 2, NBLK * 128]), ("d_v", [128, NBLK, 256]),
                        ("d_ma", [128, 8, 512]), ("d_mb", [128, 8, 512]), ("d_mg", [128, 16, 512]), ("d_h", [128, 16, 512])):
            dd[nm] = nc.dram_tensor(nm, shp, BF16, kind="ExternalOutput").ap()
    outp = nc.dram_tensor("out", [NOWN * 128, D], F32, kind="ExternalOutput").ap()

    with contextlib.ExitStack() as stk:
        S = Sched(nc, stk)

        def sb(name, shape, dt):
            return stk.enter_context(nc.sbuf_tensor(name, shape, dt))

        PS = [stk.enter_context(nc.psum_tensor(f"ps{i}", [128, 512], F32)) for i in range(8)]
        PB = [S.buf(f"ps{i}") for i in range(8)]

        slots = [sb(f"wsl{i}", [128, 16, 256], BF16) for i in range(NSLOT)]
        slotB = [S.buf(f"wsl{i}", dma=True) for i in range(NSLOT)]
        slot_ctr = [0]

        def next_slot():
            i = slot_ctr[0] % NSLOT
            slot_ctr[0] += 1
            return slots[i], slotB[i]

        Gt = [sb("G0", [128, D], F32)]
        GB = [S.buf("G0", dma=True)]
        xs = [sb(f"xs{i}", [128, D], F32) for i in range(2)]
        xsB = [S.buf(f"xs{i}", dma=True) for i in range(2)]
        xs_ctr = [0]

        def next_xs():
            i = xs_ctr[0] % 2
            xs_ctr[0] += 1
            return xs[i], xsB[i]

        xns = [sb(f"xn{i}", [128, D], BF16) for i in range(2)]
        xnsB = [S.buf(f"xn{i}") for i in range(2)]
        xn_ctr = [0]

        def next_xn():
            i = xn_ctr[0] % 2
            xn_ctr[0] += 1
            return xns[i], xnsB[i]

        hT = sb("hT", [128, 16, 512], BF16)
        hTB = [S.buf(f"hT{i}") for i in range(4)]
        ystage = sb("ystage", [128, 16, 512], F32)
        ysB = [S.buf(f"ys{i}") for i in range(16)]
        arena = sb("arena", [128, 26 * 1024], BF16)
        h1 = arena[:, 0:44 * 512].rearrange("p (f t) -> p f t", f=44)
        h1B = [S.buf(f"h1_{i}") for i in range(44)]
        def aview(off_kb, nchunk, t):
            o = off_kb * 512
            return arena[:, o:o + nchunk * t].rearrange("p (c t) -> p c t", c=nchunk)
        qT = aview(0, 8, 512)
        mbT = qT
        uT = aview(8, 8, 512)
        vn_tm = aview(16, 4, 1024)
        vnT = aview(24, 8, 512)
        mgT = aview(16, 16, 512)
        kT_all = arena[:, 32 * 512:32 * 512 + 2 * NBLK * 128].rearrange("p (g t) -> p g t", g=2)
        v_all = arena[:, 32 * 512 + 2 * NBLK * 128:32 * 512 + 4 * NBLK * 128].rearrange("p (b c) -> p b c", b=NBLK)
        qTB = [S.buf(f"qT{i}") for i in range(8)]
        uTB = [S.buf(f"uT{i}") for i in range(8)]
        vnTB = [S.buf(f"vnT{i}") for i in range(8)]
        vntmB = [S.buf(f"vntm{i}") for i in range(4)]
        mbTB = [[S.buf(f"mbT{b}_{g}") for g in range(2)] for b in range(4)]
        mgTB = [S.buf(f"mgT{i}") for i in range(16)]
        kTB = [S.buf(f"kT{i}") for i in range(NBLK)]
        vB = [S.buf(f"v{i}") for i in range(NBLK)]
        scx = arena[:, 0:2048].rearrange("p (k m) -> p k m", k=16)
        scc = arena[:, 2048:4096].rearrange("p (k m) -> p k m", k=16)
        bada_bf = arena[0:1, 4096:4096 + 2048]
        scB = S.buf("sc")
        badaB = S.buf("bada", dma=True)

        T = [sb(f"T{i}", [128, 512], F32) for i in range(4)]
        TB = [S.buf(f"T{i}") for i in range(4)]
        t_ctr = [0]

        def next_T():
            i = t_ctr[0] % 4
            t_ctr[0] += 1
            return T[i], TB[i]

        Tb = [sb(f"Tb{i}", [128, 512], BF16) for i in range(4)]
        TbB = [S.buf(f"Tb{i}") for i in range(4)]
        tb_ctr = [0]

        def next_Tb():
            i = tb_ctr[0] % 4
            tb_ctr[0] += 1
            return Tb[i], TbB[i]

        ident_f = sb("ident_f", [128, 128], F32)
        ident_bf = sb("ident_bf", [128, 128], BF16)
        ones_f = sb("ones_f", [128, 128], F32)
        ones_bf = sb("ones_bf", [128, 128], BF16)
        rmat = sb("rmat", [128, 128], BF16)
        mask_sb = sb("mask_sb", [128, 4, 128], BF16)
        wsT_sb = sb("wsT_sb", [128, 8, 128], BF16)
        bs_sb = sb("bs_sb", [1, 1024], BF16)
        es_hi = sb("es_hi", [1, 8], BF16)
        es_lo = sb("es_lo", [1, 8], BF16)
        sink_sb = sb("sink_sb", [1, 8], F32)
        lngb_sb = sb("lngb_sb", [128, 16], F32)
        cvec_sb = sb("cvec_sb", [128, 32], F32)
        sc_sb = sb("sc_sb", [128, 32], F32)
        sc_bf = sb("sc_bf", [128, 32], BF16)
        bpp = sb("bpp_sb", [128, 144], F32)
        gpp = sb("gpp_sb", [128, 96], F32)
        one11 = sb("one11", [1, 1], F32)
        NV = 14
        vecs = sb("vecs", [128, NV, 16], F32)
        vecB = [S.buf(f"vec{i}") for i in range(NV)]
        VIDX = {n: i for i, n in enumerate(["B0x", "A0x", "G0x", "B1x", "A1x", "G1x", "B2x", "A2x", "G2x",
                                            "B0c", "A0c", "G0c", "B1c", "A1c"])}
        cos_sb = sb("cos_sb", [128, 512], F32)
        sin_sb = sb("sin_sb", [128, 512], F32)
        csB = S.buf("cossin", dma=True)
        small = sb("small", [128, 32], F32)
        small2 = sb("small2", [128, 8], F32)
        small3 = sb("small3", [128, 8], F32)
        sm3B = [S.buf(f"sm3_{i}") for i in range(4)]
        smallB = [S.buf(f"small{i}") for i in range(4)]
        epsT = sb("epsT", [128, 1], F32)
        constB = S.buf("consts", dma=True)

        def act(out, in_, func, reads, writes, **kw):
            S.op("act", lambda e: e.activation(out=out, in_=in_, func=func, **kw), reads=reads, writes=writes)

        def dve_tt(out, in0, in1, op, reads, writes, eng="dve"):
            S.op(eng, lambda e: e.tensor_tensor(out=out, in0=in0, in1=in1, op=op), reads=reads, writes=writes)

        def dve_stt(out, in0, scalar, in1, op0, op1, reads, writes):
            S.op("dve", lambda e: e.scalar_tensor_tensor(out=out, in0=in0, scalar=scalar, in1=in1, op0=op0, op1=op1),
                 reads=reads, writes=writes)

        def dve_ts(out, in0, s1, s2, op0, op1, reads, writes):
            if op1 is None:
                S.op("dve", lambda e: e.tensor_scalar(out=out, in0=in0, scalar1=s1, scalar2=None, op0=op0),
                     reads=reads, writes=writes)
            else:
                S.op("dve", lambda e: e.tensor_scalar(out=out, in0=in0, scalar1=s1, scalar2=s2, op0=op0, op1=op1),
                     reads=reads, writes=writes)

        def dve_copy(out, in_, reads, writes):
            S.op("dve", lambda e: e.tensor_copy(out=out, in_=in_), reads=reads, writes=writes)

        def dve_recip(out, in_, reads, writes):
            S.op("dve", lambda e: e.reciprocal(out=out, in_=in_), reads=reads, writes=writes)

        def mm(out, pairs, reads, writes=(), pwrites=(), start=True, stop=True):
            pairs = list(pairs)

            def fn(e):
                n = len(pairs)
                ins = None
                for i, (l, r) in enumerate(pairs):
                    ins = e.matmul(out, lhsT=l, rhs=r, start=(start and i == 0), stop=(stop and i == n - 1))
                return ins
            S.op("pe", fn, reads=reads, writes=writes, pwrites=pwrites)

        def transposes(items, ident, reads, writes=(), pwrites=()):
            items = list(items)

            def fn(e):
                ins = None
                for o, i in items:
                    ins = e.transpose(o, i, ident)
                return ins
            S.op("pe", fn, reads=reads, writes=writes, pwrites=pwrites)

        def dma(q, out, in_, reads, writes, dbuf):
            S.op(q, lambda e: e.dma_start(out=out, in_=in_), reads=reads, writes=writes, dma=dbuf)

        def load_w(src_ap, nk=16):
            sl, slB = next_slot()
            dma("pool", sl[:, 0:nk, :], src_ap, [], [slB], slB)
            return sl, slB

        def dump(nm, src, rbufs):
            if not DEBUG:
                return
            if not dbgB:
                dbgB.append(S.buf("dbg", dma=True))
            db = dbgB[0]
            S.op("sp", lambda e: e.dma_start(out=dd[nm], in_=src), reads=rbufs, pwrites=[db], dma=db)

        S.op("pool", lambda e: e.memset(ident_f[:], 0.0), writes=[constB])
        S.op("pool", lambda e: e.affine_select(out=ident_f[:], in_=ident_f[:], pattern=[[-1, 128]],
                                               compare_op=ALU.not_equal, fill=1.0, base=0, channel_multiplier=1),
             reads=[constB], writes=[constB])
        S.op("dve", lambda e: e.tensor_copy(out=ident_bf[:], in_=ident_f[:]), reads=[constB], pwrites=[constB])
        S.op("dve", lambda e: e.memset(ones_f[:], 1.0), pwrites=[constB])
        S.op("dve", lambda e: e.memset(ones_bf[:], 1.0), pwrites=[constB])
        S.op("dve", lambda e: e.memset(epsT[:], EPS), pwrites=[constB])
        cB2 = S.buf("consts2", dma=True)
        for (dst, src) in ((rmat[:], rmatT), (mask_sb[:], masks), (wsT_sb[:], w_sT), (bs_sb[:], b_sp)):
            S.op("pool", (lambda e, dst=dst, src=src: e.dma_start(out=dst, in_=src)), pwrites=[cB2], dma=cB2)
        cB3 = S.buf("consts3", dma=True)
        S.op("dve", lambda e: e.memset(one11[:], 1.0), pwrites=[constB])
        for (dst, src) in ((sink_sb[:], sink), (lngb_sb[:], lngb), (cvec_sb[:], cvec), (bpp[:], bpp_d), (gpp[:], gpp_d)):
            S.op("sp", (lambda e, dst=dst, src=src: e.dma_start(out=dst, in_=src)), pwrites=[cB3], dma=cB3)
        esB = S.buf("es")
        act(sink_sb[:], sink_sb[:], AF.Exp, [cB3], [esB])
        dve_copy(es_hi[:], sink_sb[:], [esB], [esB])
        dve_tt(sink_sb[:], sink_sb[:], es_hi[:], ALU.subtract, [esB], [esB])
        dve_copy(es_lo[:], sink_sb[:], [esB], [esB])
        CONSTS = [constB, cB2, cB3, esB]

        x1B = [S.buf(f"x1_{i}", dma=True) for i in range(NBLK)]
        x2B = [S.buf(f"x2_{i}", dma=True) for i in range(NOWN)]
        outB = [S.buf(f"out_{i}", dma=True) for i in range(NOWN)]
        try:
            act(sc_sb[:], cvec_sb[:], AF.Silu, [cB3], [scB])
            dve_copy(sc_bf[:], sc_sb[:], [scB], [scB])
            w_ada_v = w_ada.rearrange("(kc p) n -> p kc n", p=128)
            vec_started = set()

            def mod_quarter(j, q4, streams):
                kind = j % 3
                sub = j // 3
                col0 = j * D + q4 * 512
                s0, s0B = load_w(w_ada_v[:, :, col0:col0 + 256])
                s1, s1B = load_w(w_ada_v[:, :, col0 + 256:col0 + 512])
                c0 = q4 * 4
                for si, (sn, off) in enumerate(streams):
                    bank_r = 2 + 2 * si
                    bank_c = 3 + 2 * si
                    mm(PS[bank_r][0:1, 0:256], [(sc_bf[:, off + kc:off + kc + 1], s0[:, kc, :]) for kc in range(16)],
                       [scB, s0B], writes=[PB[bank_r]])
                    mm(PS[bank_r][0:1, 256:512], [(sc_bf[:, off + kc:off + kc + 1], s1[:, kc, :]) for kc in range(16)],
                       [scB, s1B], pwrites=[PB[bank_r]])
                    row, rowB = next_T()
                    act(row[0:1, :], PS[bank_r][0:1, :], AF.Copy, [PB[bank_r]], [rowB])
                    for q in range(4):
                        mm(PS[bank_c][:, q:q + 1], [(row[0:1, q * 128:(q + 1) * 128], one11[0:1, 0:1])], [rowB, constB],
                           writes=[PB[bank_c]] if q == 0 else (), pwrites=() if q == 0 else [PB[bank_c]])
                    nm = ("B", "A", "G")[kind] + str(sub) + sn
                    vi = VIDX[nm]
                    first = nm not in vec_started
                    vec_started.add(nm)
                    wr = [vecB[vi]] if first else []
                    pw = [] if first else [vecB[vi]]
                    dst = vecs[:, vi, c0:c0 + 4]
                    bq_ = bpp[:, j * 16 + c0:j * 16 + c0 + 4]
                    pc = PS[bank_c][:, 0:4]
                    if kind == 0:
                        S.op("dve", (lambda e, dst=dst, pc=pc, bq_=bq_: e.tensor_tensor(out=dst, in0=pc, in1=bq_, op=ALU.add)),
                             reads=[PB[bank_c], cB3], writes=wr, pwrites=pw)
                    else:
                        gq_ = gpp[:, (2 * sub + (kind - 1)) * 16 + c0:(2 * sub + (kind - 1)) * 16 + c0 + 4]
                        tq, tqB = next_T()
                        if kind == 1:
                            S.op("dve", (lambda e, tq=tq, pc=pc, bq_=bq_: e.scalar_tensor_tensor(
                                out=tq[:, 0:4], in0=pc, scalar=1.0, in1=bq_, op0=ALU.add, op1=ALU.add)),
                                reads=[PB[bank_c], cB3], writes=[tqB])
                            S.op("dve", (lambda e, dst=dst, tq=tq, gq_=gq_: e.tensor_tensor(out=dst, in0=tq[:, 0:4], in1=gq_, op=ALU.mult)),
                                 reads=[tqB, cB3], writes=wr, pwrites=pw)
                        else:
                            coef = 1.0 if sub == 1 else 0.5
                            S.op("dve", (lambda e, tq=tq, pc=pc, bq_=bq_: e.tensor_tensor(out=tq[:, 0:4], in0=pc, in1=bq_, op=ALU.add)),
                                 reads=[PB[bank_c], cB3], writes=[tqB])
                            S.op("dve", (lambda e, dst=dst, tq=tq, gq_=gq_, coef=coef: e.scalar_tensor_tensor(
                                out=dst, in0=tq[:, 0:4], scalar=coef, in1=gq_, op0=ALU.mult, op1=ALU.mult)),
                                reads=[tqB, cB3], writes=wr, pwrites=pw)

            for j in range(2):
                for q4 in range(4):
                    mod_quarter(j, q4, [("x", 0), ("c", 16)])
            ffn_extra = [(lambda j=j, q4=q4: mod_quarter(j, q4, [("x", 0), ("c", 16)])) for j in range(2, 5) for q4 in range(4)]
            deferred_mod = [(lambda j=j, q4=q4: mod_quarter(j, q4, [("x", 0)])) for j in range(5, 9) for q4 in range(4)]

            _ck('mod')
            pass

            def expand_vec(vname, gi=0):
                vi = VIDX[vname]
                for q4 in range(4):
                    bank = 4 + q4
                    for c4 in range(4):
                        c = q4 * 4 + c4
                        tt, ttB = next_T()
                        dve_ts(tt[:, 0:128], ident_f[:], vecs[:, vi, c:c + 1], None, ALU.mult, None, [constB, vecB[vi]], [ttB])
                        if c4 == 0:
                            mm(PS[bank][:, c4 * 128:(c4 + 1) * 128], [(ones_f[:], tt[:, 0:128])], [ttB, constB], writes=[PB[bank]])
                        else:
                            mm(PS[bank][:, c4 * 128:(c4 + 1) * 128], [(ones_f[:], tt[:, 0:128])], [ttB, constB], pwrites=[PB[bank]])
                    cols = slice(q4 * 512, (q4 + 1) * 512)
                    if q4 == 0:
                        act(Gt[gi][:, cols], PS[bank][:], AF.Copy, [PB[bank]], [GB[gi]])
                    else:
                        S.op("act", (lambda e, cols=cols, bank=bank: e.activation(out=Gt[gi][:, cols], in_=PS[bank][:], func=AF.Copy)),
                             reads=[PB[bank]], pwrites=[GB[gi]])

            pre_state = {}

            def pre_block(src_ap, srcB, bi, sfx, sub):
                pre_a(src_ap, srcB, bi)
                pre_b(bi, sfx, sub)

            def pre_seq(items):
                a = [(lambda it=it: pre_a(it[0], it[1], it[2])) for it in items]
                b = [(lambda it=it: pre_b(it[2], it[3], it[4])) for it in items]
                out = []
                n = len(items)
                for k in range(n + 1):
                    if k < n:
                        out.append(a[k])
                    if k >= 1:
                        out.append(b[k - 1])
                return out

            def pre_a(src_ap, srcB, bi):
                xt, xtB = next_xs()
                dma("sp", xt[:], src_ap, [srcB] if srcB is not None else [], [xtB], xtB)
                sm = small[:, bi * 4:bi * 4 + 4]
                smB = smallB[bi]
                for q4 in range(4):
                    jt, jtB = next_Tb()
                    S.op("act", (lambda e, jt=jt, q4=q4, xt=xt, sm=sm: e.activation(
                        out=jt[:], in_=xt[:, q4 * 512:(q4 + 1) * 512], func=AF.Square, accum_out=sm[:, q4:q4 + 1])),
                        reads=[xtB], writes=[jtB, smB] if q4 == 0 else [jtB], pwrites=[] if q4 == 0 else [smB])
                s3 = small3[:, bi:bi + 1]
                r3 = small3[:, 4 + bi:5 + bi]
                S.op("dve", (lambda e, s3=s3, sm=sm: e.tensor_reduce(out=s3, in_=sm, axis=AX.X, op=ALU.add)),
                     reads=[smB], writes=[sm3B[bi]])
                act(s3, s3, AF.Sqrt, [sm3B[bi], constB], [sm3B[bi]], scale=1.0 / D, bias=epsT[:])
                dve_recip(r3, s3, [sm3B[bi]], [sm3B[bi]])
                xn, xnB = next_xn()
                dve_ts(xn[:], xt[:], r3, None, ALU.mult, None, [xtB, sm3B[bi]], [xnB])
                pre_state[bi] = (xn, xnB)

            def pre_b(bi, sfx, sub):
                xn, xnB = pre_state[bi]
                vA = VIDX["A%d%s" % (sub, sfx)]
                vBi = VIDX["B%d%s" % (sub, sfx)]
                for hb in range(2):
                    bank = hb
                    pv = PS[bank][:].bitcast(BF16).rearrange("p (c t) -> p c t", c=8)
                    transposes([(pv[:, c, :], xn[:, (hb * 8 + c) * 128:(hb * 8 + c + 1) * 128]) for c in range(8)],
                               ident_bf[:], [xnB, constB], writes=[PB[bank]])
                    for c in range(8):
                        cc = hb * 8 + c
                        wr = [hTB[bi]] if (hb == 0 and c == 0) else []
                        pw = [] if (hb == 0 and c == 0) else [hTB[bi]]
                        S.op("act", (lambda e, pv=pv, c=c, cc=cc: e.activation(
                            out=hT[:, cc, bi * 128:(bi + 1) * 128], in_=pv[:, c, :], func=AF.Identity,
                            scale=vecs[:, vA, cc:cc + 1], bias=vecs[:, vBi, cc:cc + 1])),
                            reads=[PB[bank], vecB[vA], vecB[vBi]], writes=wr, pwrites=pw)

            def post_block(bi, nb_cols, src_ap, srcB, dst_ap, dstB, gi=0, banks=(2, 3, 4, 5)):
                xt, xtB = next_xs()
                dma("sp", xt[:], src_ap, [srcB] if srcB is not None else [], [xtB], xtB)
                sm = small[:, 16 + bi * 4:16 + bi * 4 + 4]
                smB = smallB[bi]
                jt, jtB = next_T()
                for q4 in range(4):
                    bank = banks[q4]
                    transposes([(PS[bank][:, c4 * 128:(c4 + 1) * 128], ystage[:, q4 * 4 + c4, bi * 128:(bi + 1) * 128]) for c4 in range(4)],
                               ident_f[:], [ysB[q4 * 4 + c4] for c4 in range(4)] + [constB], writes=[PB[bank]])
                    S.op("act", (lambda e, bank=bank, q4=q4, jt=jt, sm=sm: e.activation(out=jt[:], in_=PS[bank][:], func=AF.Square,
                                                                                     accum_out=sm[:, q4:q4 + 1])),
                         reads=[PB[bank]], writes=[jtB, smB] if q4 == 0 else [jtB], pwrites=[] if q4 == 0 else [smB])
                st2 = small2[:, bi:bi + 1]
                S.op("dve", lambda e: e.tensor_reduce(out=st2, in_=sm, axis=AX.X, op=ALU.add), reads=[smB], writes=[smB])
                act(st2, st2, AF.Sqrt, [smB, constB], [smB], scale=1.0 / D, bias=epsT[:])
                dve_recip(st2, st2, [smB], [smB])
                for q4 in range(4):
                    bank = banks[q4]
                    cols = slice(q4 * 512, (q4 + 1) * 512)
                    tt, ttB = next_T()
                    dve_stt(tt[:], PS[bank][:], st2, Gt[gi][:, cols], ALU.mult, ALU.mult, [PB[bank], smB, GB[gi]], [ttB])
                    if q4 == 0:
                        dve_tt(xt[:, cols], xt[:, cols], tt[:], ALU.add, [ttB, xtB], [xtB])
                    else:
                        S.op("dve", (lambda e, cols=cols, tt=tt: e.tensor_tensor(out=xt[:, cols], in0=xt[:, cols], in1=tt[:], op=ALU.add)),
                             reads=[ttB], pwrites=[xtB])
                dma("sp", dst_ap, xt[:], [xtB], [dstB], dstB)
                if DEBUG and dst_ap.tensor.name in ("x1", "x2"):
                    dd = x1d if dst_ap.tensor.name == "x1" else x2d
                    if not dbgB:
                        dbgB.append(S.buf("dbg", dma=True))
                    db = dbgB[0]
                    r0 = dst_ap.offset // D
                    S.op("sp", (lambda e, dd=dd, r0=r0, xt=xt: e.dma_start(out=dd[r0:r0 + 128, :], in_=xt[:])), reads=[xtB], pwrites=[db], dma=db)

            def ffn(l, sub, tiles, src_of, dst_of, tail_sec=None, skip_first_pre=False):
                wi = w_ffn_in[l].rearrange("(kc p) n -> p kc n", p=128)
                wo = w_ffn_out[l].rearrange("(f p) n -> p f n", p=128)
                cur_g = [None]

                def need_g(sfx):
                    nm = "G%d%s" % (sub, sfx)
                    if cur_g[0] != nm:
                        expand_vec(nm)
                        cur_g[0] = nm

                def pre_ops(blocks):
                    items = []
                    for bi, blk in enumerate(blocks):
                        sap, sB, sfx = src_of(blk)
                        items.append((sap, sB, bi, sfx, sub))
                    return pre_seq(items)

                def post_ops(blocks):
                    def mk(bi, blk):
                        def f():
                            sap, sB, sfx = src_of(blk)
                            dap, dB = dst_of(blk)
                            need_g(sfx)
                            post_block(bi, len(blocks) * 128, sap, sB, dap, dB, 0, banks=(0, 1, 6, 7))
                        return f
                    return [mk(bi, blk) for bi, blk in enumerate(blocks)]

                def A_ops(blocks):
                    nb = len(blocks)
                    tt_ = nb * 128
                    hreads = [hTB[bi] for bi in range(nb)]

                    def mk(fp):
                        def f():
                            sg_, sgB = load_w(wi[:, :, fp * 256:(fp + 1) * 256])
                            su_, suB = load_w(wi[:, :, DFF + fp * 256:DFF + (fp + 1) * 256])
                            for s2 in range(2):
                                ff = fp * 2 + s2
                                bg = 2 + (ff % 2) * 2
                                bu = bg + 1
                                mm(PS[bg][:, 0:tt_], [(sg_[:, kc, s2 * 128:(s2 + 1) * 128], hT[:, kc, 0:tt_]) for kc in range(16)],
                                   [sgB] + hreads, writes=[PB[bg]])
                                mm(PS[bu][:, 0:tt_], [(su_[:, kc, s2 * 128:(s2 + 1) * 128], hT[:, kc, 0:tt_]) for kc in range(16)],
                                   [suB] + hreads, writes=[PB[bu]])
                                tt, ttB = next_T()
                                act(tt[:, 0:tt_], PS[bg][:, 0:tt_], AF.Silu, [PB[bg]], [ttB])
                                dve_tt(h1[:, ff, 0:tt_], tt[:, 0:tt_], PS[bu][:, 0:tt_], ALU.mult, [ttB, PB[bu]], [h1B[ff]])
                        return f
                    return [mk(fp) for fp in range(22)]

                def B_ops(blocks):
                    nb = len(blocks)
                    tt_ = nb * 128

                    def mk(cp):
                        yb0 = 6 if cp % 2 == 0 else 4

                        def f():
                            for j, (f0, nk) in enumerate(((0, 16), (16, 16), (32, 12))):
                                sl, slB = load_w(wo[:, f0:f0 + nk, cp * 256:(cp + 1) * 256], nk)
                                for s2 in range(2):
                                    by = yb0 + s2
                                    pairs = [(sl[:, fi, s2 * 128:(s2 + 1) * 128], h1[:, f0 + fi, 0:tt_]) for fi in range(nk)]
                                    mm(PS[by][:, 0:tt_], pairs, [slB] + h1B[f0:f0 + nk],
                                       writes=[PB[by]] if j == 0 else (), pwrites=() if j == 0 else [PB[by]],
                                       start=(j == 0), stop=(j == 2))
                            for s2 in range(2):
                                c = cp * 2 + s2
                                act(ystage[:, c, 0:tt_], PS[yb0 + s2][:, 0:tt_], AF.Copy, [PB[yb0 + s2]], [ysB[c]])
                            if ffn_extra:
                                ffn_extra.pop(0)()
                        return f
                    return [mk(cp) for cp in range(8)]

                def interleave(prim, sec, frac=1.0):
                    n, m = len(prim), len(sec)
                    pos = {}
                    for k in range(m):
                        p = min(n - 1, int((k + 0.5) * n * frac / m))
                        pos.setdefault(p, []).append(sec[k])
                    for i, op_ in enumerate(prim):
                        op_()
                        for o in pos.get(i, []):
                            o()

                if not skip_first_pre:
                    for o in pre_ops(tiles[0]):
                        o()
                for t, blocks in enumerate(tiles):
                    if t == 0:
                        _ck('pre')
                    interleave(A_ops(blocks), post_ops(tiles[t - 1]) if t > 0 else [], 0.7)
                    if t == 0:
                        _ck('phaseA')
                    interleave(B_ops(blocks), pre_ops(tiles[t + 1]) if t + 1 < len(tiles) else (tail_sec or []), 1.0)
                    if t == 0:
                        _ck('phaseB')
                for o in post_ops(tiles[-1]):
                    o()


            def blkrows(ap, b):
                return ap[b * 128:(b + 1) * 128, :]

            tiles0 = [[0, 1, 2, 3], [4, 5, 6, 7], [8, 9, 10, 11], [12, 13, 14, 15], [16, 17, 18]]
            kvA = [4, 8, 12]
            ffn(0, 0, tiles0,
                lambda b: (blkrows(xe, b), None, "c" if b >= CTX0 else "x"),
                lambda b: (blkrows(x1, b), x1B[b]),
                tail_sec=pre_seq([(blkrows(x1, blk), x1B[blk], bi, "x", 1) for bi, blk in enumerate(kvA)]))
            _ck('ffn0')
            pass

            w_in_v = w_in.rearrange("(kc p) n -> p kc n", p=128)

            def kv_compute(blocks, wk_, wkB, wv_, wvB, skip=()):
                nb = len(blocks)
                tt_ = nb * 128
                hreads = [hTB[bi] for bi in range(nb)]
                for g in range(2):
                    bk = 2 + g
                    mm(PS[bk][:, 0:tt_], [(wk_[:, kc, g * 128:(g + 1) * 128], hT[:, kc, 0:tt_]) for kc in range(16)],
                       [wkB] + hreads, writes=[PB[bk]])
                    kb_, kbB = next_Tb()
                    dve_copy(kb_[:, 0:tt_], PS[bk][:, 0:tt_], [PB[bk]], [kbB])
                    br = 4 + g
                    mm(PS[br][:, 0:tt_], [(rmat[:], kb_[:, 0:tt_])], [kbB, cB2], writes=[PB[br]])
                    for bi, blk in enumerate(blocks):
                        if blk in skip:
                            continue
                        cs = slice(bi * 128, (bi + 1) * 128)
                        dst = kT_all[:, g, blk * 128:(blk + 1) * 128]
                        wr = [kTB[blk]] if g == 0 else []
                        pw = [] if g == 0 else [kTB[blk]]
                        if blk < CTX0:
                            t1, t1B = next_T()
                            t2, t2B = next_T()
                            dve_tt(t1[:, 0:128], PS[bk][:, cs], cos_sb[:, cs], ALU.mult, [PB[bk], csB], [t1B])
                            dve_tt(t2[:, 0:128], PS[br][:, cs], sin_sb[:, cs], ALU.mult, [PB[br], csB], [t2B])
                            S.op("dve", (lambda e, dst=dst, t1=t1, t2=t2: e.tensor_tensor(out=dst, in0=t1[:, 0:128], in1=t2[:, 0:128], op=ALU.add)),
                                 reads=[t1B, t2B], writes=wr, pwrites=pw)
                        else:
                            S.op("dve", (lambda e, dst=dst, bk=bk, cs=cs: e.tensor_copy(out=dst, in_=PS[bk][:, cs])),
                                 reads=[PB[bk]], writes=wr, pwrites=pw)
                for bi, blk in enumerate(blocks):
                    if blk in skip:
                        continue
                    bv = 6 + (bi % 2)
                    mm(PS[bv][:, 0:256], [(hT[:, kc, bi * 128:(bi + 1) * 128], wv_[:, kc, :]) for kc in range(16)],
                       [wvB, hTB[bi]], writes=[PB[bv]])
                    act(v_all[:, blk, :], PS[bv][:, 0:256], AF.Copy, [PB[bv]], [vB[blk]])

            wk0, wk0B = load_w(w_in_v[:, :, 1024:1280])
            wv0, wv0B = load_w(w_in_v[:, :, 1280:1536])
            pre_done = set()
            for bi, blk in enumerate(kvA):
                S.op("sp", (lambda e, bi=bi, blk=blk: e.dma_start(out=cos_sb[:, bi * 128:(bi + 1) * 128], in_=cosT[:, blk * 128:(blk + 1) * 128])),
                     writes=[csB] if bi == 0 else (), pwrites=() if bi == 0 else [csB], dma=csB)
                S.op("sp", (lambda e, bi=bi, blk=blk: e.dma_start(out=sin_sb[:, bi * 128:(bi + 1) * 128], in_=sinT[:, blk * 128:(blk + 1) * 128])),
                     pwrites=[csB], dma=csB)
            kv_compute(kvA, wk0, wk0B, wv0, wv0B)
            pre_done.update(kvA)
            for blocks in ([16, 17, 18],):
                nb = len(blocks)
                for o_ in pre_seq([(blkrows(x1, blk), x1B[blk], bi, "c" if blk >= CTX0 else "x", 1) for bi, blk in enumerate(blocks)]):
                    o_()
                t0 = blocks[0] * 128
                nrope = sum(1 for b in blocks if b < CTX0)
                dma("sp", cos_sb[:, 0:nrope * 128], cosT[:, t0:t0 + nrope * 128], [], [csB], csB)
                S.op("sp", (lambda e, t0=t0, nrope=nrope: e.dma_start(out=sin_sb[:, 0:nrope * 128], in_=sinT[:, t0:t0 + nrope * 128])),
                     pwrites=[csB], dma=csB)
                kv_compute(blocks, wk0, wk0B, wv0, wv0B)
                pre_done.update(blocks)

            _ck('kv')
            tilesM = [[0, 1, 2, 3], [4, 5, 6, 7], [8, 9, 10, 11], [12, 13, 14, 15]]
            lng = lngb_sb[:, 0:8]
            lnb = lngb_sb[:, 8:16]
            pend_post = []

            def run_post(k=1):
                for _ in range(k):
                    if pend_post:
                        pend_post.pop(0)()

            for ti, blocks in enumerate(tilesM):
                nb = 4
                tt_ = 512
                t0 = blocks[0] * 128
                if ti == 0:
                    for o_ in pre_seq([(blkrows(x1, blk), x1B[blk], bi, "x", 1) for bi, blk in enumerate(blocks)]):
                        o_()
                if ti == 1:
                    expand_vec("G1x")
                hreads = [hTB[bi] for bi in range(4)]
                dma("sp", cos_sb[:], cosT[:, t0:t0 + 512], [], [csB], csB)
                S.op("sp", (lambda e, t0=t0: e.dma_start(out=sin_sb[:], in_=sinT[:, t0:t0 + 512])), pwrites=[csB], dma=csB)

                def proj_chunk(sl, slB, s2, bank):
                    mm(PS[bank][:], [(sl[:, kc, s2 * 128:(s2 + 1) * 128], hT[:, kc, :]) for kc in range(16)],
                       [slB] + hreads, writes=[PB[bank]])

                pend_rot = []

                def q_finish(h, bq, br, qb_, qbB, t1, t1B):
                    mm(PS[br][:], [(rmat[:], qb_[:])], [qbB, cB2], writes=[PB[br]])
                    t2, t2B = next_T()
                    dve_tt(t2[:], PS[br][:], sin_sb[:], ALU.mult, [PB[br], csB], [t2B])
                    dve_tt(qT[:, h, :], t1[:], t2[:], ALU.add, [t1B, t2B], [qTB[h]])

                for cp in range(4):
                    sl, slB = load_w(w_in_v[:, :, cp * 256:(cp + 1) * 256])
                    for s2 in range(2):
                        h = cp * 2 + s2
                        bq = 2 + (h % 2)
                        br = 4 + (h % 2)
                        proj_chunk(sl, slB, s2, bq)
                        qb_, qbB = next_Tb()
                        dve_copy(qb_[:], PS[bq][:], [PB[bq]], [qbB])
                        t1, t1B = next_T()
                        dve_tt(t1[:], PS[bq][:], cos_sb[:], ALU.mult, [PB[bq], csB], [t1B])
                        if pend_rot:
                            pend_rot.pop(0)()
                        pend_rot.append(lambda h=h, bq=bq, br=br, qb_=qb_, qbB=qbB, t1=t1, t1B=t1B: q_finish(h, bq, br, qb_, qbB, t1, t1B))
                    while pend_rot:
                        pend_rot.pop(0)()
                    run_post(1)
                wkt, wktB = load_w(w_in_v[:, :, 1024:1280])
                wvt, wvtB = load_w(w_in_v[:, :, 1280:1536])
                kv_compute(blocks, wkt, wktB, wvt, wvtB, skip=pre_done)
                run_post(4)
                _ck('mq')
                vgf = ystage
                pend_stats = []
                for cp in range(4):
                    sl, slB = load_w(w_in_v[:, :, 2560 + cp * 256:2560 + (cp + 1) * 256])
                    for s2 in range(2):
                        c = cp * 2 + s2
                        bq = 2 + (c % 2)
                        proj_chunk(sl, slB, s2, bq)
                        act(vgf[:, c, :], PS[bq][:], AF.Gelu_apprx_tanh, [PB[bq]], [ysB[c]])
                        sq, sqB = next_T()
                        act(sq[:], vgf[:, c, :], AF.Square, [ysB[c]], [sqB])
                        if pend_stats:
                            pend_stats.pop(0)()

                        def stats(c=c, sq=sq, sqB=sqB):
                            mm(PS[6][:], [(ones_f[:], vgf[:, c, :])], [ysB[c], constB], writes=[PB[6]] if c == 0 else (),
                               pwrites=() if c == 0 else [PB[6]], start=(c == 0), stop=(c == 7))
                            mm(PS[7][:], [(ones_f[:], sq[:])], [sqB, constB], writes=[PB[7]] if c == 0 else (),
                               pwrites=() if c == 0 else [PB[7]], start=(c == 0), stop=(c == 7))
                        pend_stats.append(stats)
                while pend_stats:
                    pend_stats.pop(0)()
                mu = cos_sb
                var = sin_sb
                act(mu[:], PS[6][:], AF.Identity, [PB[6]], [csB], scale=1.0 / 1024)
                msq, msqB = next_T()
                dve_tt(msq[:], mu[:], mu[:], ALU.mult, [csB], [msqB])
                S.op("dve", lambda e, msq=msq, var=var: e.scalar_tensor_tensor(out=var[:], in0=PS[7][:], scalar=1.0 / 1024, in1=msq[:],
                                                             op0=ALU.mult, op1=ALU.subtract),
                     reads=[PB[7], msqB], pwrites=[csB])
                S.op("act", lambda e, var=var: e.activation(out=var[:], in_=var[:], func=AF.Sqrt, bias=epsT[:]),
                     reads=[csB, constB], pwrites=[csB])
                S.op("dve", lambda e, var=var: e.reciprocal(out=var[:], in_=var[:]), reads=[csB], pwrites=[csB])
                _ck('mvg')

                u_state = {}

                def u_chunk(c):
                    if c % 2 == 0:
                        u_state["sl"] = load_w(w_in_v[:, :, 1536 + (c // 2) * 256:1536 + (c // 2 + 1) * 256])
                    sl, slB = u_state["sl"]
                    bq = c % 2
                    proj_chunk(sl, slB, c % 2, bq)
                    act(uT[:, c, :], PS[bq][:], AF.Gelu_apprx_tanh, [PB[bq]], [uTB[c]])

                def norm_pair(c0):
                    for c in (c0, c0 + 1):
                        tt, ttB = next_T()
                        dve_tt(tt[:], vgf[:, c, :], mu[:], ALU.subtract, [ysB[c], csB], [ttB])
                        dve_tt(tt[:], tt[:], var[:], ALU.mult, [ttB, csB], [ttB])
                        act(vnT[:, c, :], tt[:], AF.Identity, [ttB, cB3], [vnTB[c]], scale=lng[:, c:c + 1], bias=lnb[:, c:c + 1])

                def att_unit(bi, qb, g, mid=None):
                    prev = (qb - 1, 0) if qb > 0 else (HALO, 2)
                    nxt = (qb + 1, 1) if qb < NOWN - 1 else (HALO, 3)
                    kbs = [(prev[0], prev[1]), (qb, None), (nxt[0], nxt[1]), (CTX0, None), (CTX0 + 1, None)]
                    qv = qT[:, 4 * g:4 * g + 4, bi * 128:(bi + 1) * 128]
                    qreads = qTB[4 * g:4 * g + 4]
                    bO, bD = 5, 6
                    sbanks = [2, 3, 4]

                    def s_mm(i):
                        kb, mi = kbs[i]
                        bank = sbanks[i % 3]
                        pairs = [(kT_all[:, g, kb * 128:(kb + 1) * 128], qv)]
                        if mi is not None:
                            pairs.append((ident_bf[:], mask_sb[:, mi, :].unsqueeze(1).to_broadcast([128, 4, 128])))
                        mm(PS[bank][:].rearrange("p (h t) -> p h t", h=4), pairs,
                           [kTB[kb], cB2, constB] + qreads, writes=[PB[bank]])
                    s_mm(0)
                    s_mm(1)
                    s_mm(2)
                    if mid is not None:
                        mid()
                    for i in range(5):
                        kb, mi = kbs[i]
                        bank = sbanks[i % 3]
                        pT, pTB = next_Tb()
                        act(pT[:], PS[bank][:], AF.Exp, [PB[bank]], [pTB], scale=SCALE)
                        mm(PS[bO][:], [(v_all[:, kb, g * 128:(g + 1) * 128], pT[:])], [vB[kb], pTB],
                           writes=[PB[bO]] if i == 0 else (), pwrites=() if i == 0 else [PB[bO]],
                           start=(i == 0), stop=(i == 4))
                        pairsD = [(ones_bf[:], pT[:].rearrange("p (h t) -> p h t", h=4))]
                        if i == 4:
                            pairsD.append((ones_bf[0:1, :], es_hi[0:1, 4 * g:4 * g + 4].unsqueeze(2).to_broadcast([1, 4, 128])))
                            pairsD.append((ones_bf[0:1, :], es_lo[0:1, 4 * g:4 * g + 4].unsqueeze(2).to_broadcast([1, 4, 128])))
                        mm(PS[bD][:].rearrange("p (h t) -> p h t", h=4), pairsD, [pTB, constB, esB],
                           writes=[PB[bD]] if i == 0 else (), pwrites=() if i == 0 else [PB[bD]],
                           start=(i == 0), stop=(i == 4))
                        if i + 3 < 5:
                            s_mm(i + 3)
                    rD, rDB = next_T()
                    dve_recip(rD[:], PS[bD][:], [PB[bD]], [rDB])
                    ov = mbT[:, 4 * g:4 * g + 4, bi * 128:(bi + 1) * 128]
                    S.op("dve", (lambda e, ov=ov, rD=rD, bO=bO: e.tensor_tensor(
                        out=ov, in0=PS[bO][:].rearrange("p (h t) -> p h t", h=4),
                        in1=rD[:].rearrange("p (h t) -> p h t", h=4), op=ALU.mult)),
                        reads=[PB[bO], rDB], writes=[mbTB[bi][g]])

                units = [(bi, qb, g) for bi, qb in enumerate(blocks) for g in range(2)]
                for ui, (bi, qb, g) in enumerate(units):
                    att_unit(bi, qb, g, mid=(lambda ui=ui: u_chunk(ui)))
                    if ui % 2 == 0:
                        norm_pair(ui)
                _ck('matt')
                for bi in range(4):
                    bank = bi % 2
                    pv = PS[bank][:].bitcast(BF16)
                    transposes([(pv[:, c * 128:(c + 1) * 128], vnT[:, c, bi * 128:(bi + 1) * 128]) for c in range(8)],
                               ident_bf[:], vnTB + [constB], writes=[PB[bank]])
                    dve_copy(vn_tm[:, bi, :], pv[:, :], [PB[bank]], [vntmB[bi]])
                for bi in range(4):
                    for hb in range(2):
                        bank = 4 + hb
                        for g4 in range(4):
                            g = hb * 4 + g4
                            o = PS[bank][:, g4 * 128:(g4 + 1) * 128]
                            pairs = [(vn_tm[:, bi, g * 128:(g + 1) * 128], wsT_sb[:, g, :]),
                                     (ones_bf[0:1, :], bs_sb[0:1, g * 128:(g + 1) * 128])]
                            if g4 == 0:
                                mm(o, pairs, [vntmB[bi], cB2, constB], writes=[PB[bank]])
                            else:
                                mm(o, pairs, [vntmB[bi], cB2, constB], pwrites=[PB[bank]])
                        uv = uT[:, hb * 4:hb * 4 + 4, bi * 128:(bi + 1) * 128]
                        S.op("dve", (lambda e, uv=uv, bank=bank: e.tensor_tensor(
                            out=uv, in0=uv, in1=PS[bank][:].rearrange("p (g t) -> p g t", g=4), op=ALU.mult)),
                            reads=[PB[bank]] + uTB[hb * 4:hb * 4 + 4], pwrites=uTB[hb * 4:hb * 4 + 4])
                mbreads = [mbTB[bi][g] for bi in range(4) for g in range(2)]
                _ck('msp')
                if blocks[0] == 0:
                    dump("d_q", qT, qTB)
                    dump("d_k", kT_all, kTB)
                    dump("d_v", v_all, vB)
                    dump("d_h", hT[:], hreads)
                wa_v = w_bra.rearrange("(kc p) n -> p kc n", p=128)
                wb_v = w_brb.rearrange("(kc p) n -> p kc n", p=128)
                for cp in range(8):
                    sga, sgaB = load_w(w_in_v[:, :, 3584 + cp * 256:3584 + (cp + 1) * 256])
                    sgb, sgbB = load_w(w_in_v[:, :, 5632 + cp * 256:5632 + (cp + 1) * 256])
                    for s2 in range(2):
                        c = cp * 2 + s2
                        if s2 == 0:
                            swa, swaB = load_w(wa_v[:, :, cp * 256:(cp + 1) * 256], 8)
                            swb, swbB = load_w(wb_v[:, :, cp * 256:(cp + 1) * 256], 8)
                        b_ga, b_gb, b_pa, b_pb = (2, 3, 4, 5) if c % 2 == 0 else (6, 7, 0, 1)
                        proj_chunk(sga, sgaB, s2, b_ga)
                        proj_chunk(sgb, sgbB, s2, b_gb)
                        mm(PS[b_pa][:], [(swa[:, kc, s2 * 128:(s2 + 1) * 128], uT[:, kc, :]) for kc in range(8)],
                           [swaB] + uTB, writes=[PB[b_pa]])
                        mm(PS[b_pb][:], [(swb[:, kc, s2 * 128:(s2 + 1) * 128], mbT[:, kc, :]) for kc in range(8)],
                           [swbB] + mbreads, writes=[PB[b_pb]])
                        ta, taB = next_T()
                        tb, tbB = next_T()
                        act(ta[:], PS[b_ga][:], AF.Sigmoid, [PB[b_ga]], [taB])
                        act(tb[:], PS[b_gb][:], AF.Sigmoid, [PB[b_gb]], [tbB])
                        dve_tt(ta[:], ta[:], PS[b_pa][:], ALU.mult, [taB, PB[b_pa]], [taB])
                        dve_tt(tb[:], tb[:], PS[b_pb][:], ALU.mult, [tbB, PB[b_pb]], [tbB])
                        dve_tt(mgT[:, c, :], ta[:], tb[:], ALU.add, [taB, tbB], [mgTB[c]] + vnTB + vntmB)
                if blocks[0] == 0:
                    dump("d_ma", uT, uTB)
                    dump("d_mb", mbT, mbreads)
                    dump("d_mg", mgT, mgTB)
                _ck('mmerge')
                nxt_blocks = tilesM[ti + 1] if ti + 1 < len(tilesM) else []
                if nxt_blocks:
                    nxt_pre = pre_seq([(blkrows(x1, b_), x1B[b_], nbi, "x", 1) for nbi, b_ in enumerate(nxt_blocks)])
                else:
                    nxt_pre = pre_seq([(blkrows(x2, b_), x2B[b_], nbi, "x", 2) for nbi, b_ in enumerate(tilesM[0])])
                for cp in range(8):
                    sl, slB = load_w(w_o.rearrange("(kc p) n -> p kc n", p=128)[:, :, cp * 256:(cp + 1) * 256])
                    for s2 in range(2):
                        c = cp * 2 + s2
                        by = 6 + (c % 2)
                        mm(PS[by][:], [(sl[:, kc, s2 * 128:(s2 + 1) * 128], mgT[:, kc, :]) for kc in range(16)],
                           [slB] + mgTB, writes=[PB[by]])
                        act(ystage[:, c, :], PS[by][:], AF.Copy, [PB[by]], [ysB[c]])
                    if nxt_pre:
                        nxt_pre.pop(0)()
                    if cp % 2 == 0 and deferred_mod:
                        deferred_mod.pop(0)()
                while nxt_pre:
                    nxt_pre.pop(0)()
                for bi, blk in enumerate(blocks):
                    pend_post.append(lambda bi=bi, blk=blk: post_block(bi, 512, blkrows(x1, blk), x1B[blk], blkrows(x2, blk), x2B[blk],
                                                                        0, banks=(0, 1, 6, 7)))
                _ck('mtile0')
            while deferred_mod:
                deferred_mod.pop(0)()
            run_post(8)
            _ck('mixer')
            pass

            ffn(1, 2, tilesM,
                lambda b: (blkrows(x2, b), x2B[b], "x"),
                lambda b: (blkrows(outp, b), outB[b]), skip_first_pre=True)


        except _Stop:
            pass
        S.wait_all("sp", outB + x1B + x2B + dbgB)
        S.emit()
    return nc


def _rope_tables(positions):
    pos = np.asarray(positions, dtype=np.float32)
    row = np.floor(pos / 64.0).astype(np.float32)
    col = (pos - row * 64.0).astype(np.float32)
    freqs = (np.float32(10000.0) ** (-np.arange(32, dtype=np.float32) / np.float32(32))).astype(np.float32)
    ang_r = row[:, None] * freqs[None, :]
    ang_c = col[:, None] * freqs[None, :]
    ang = np.concatenate([ang_r, ang_r, ang_c, ang_c], axis=-1).astype(np.float32)
    return np.ascontiguousarray(np.cos(ang).T.astype(np.float32)), np.ascontiguousarray(np.sin(ang).T.astype(np.float32))


def _const_tables(half):
    RT = np.zeros((128, 128), np.float32)
    for m in range(128):
        if (m % 64) < 32:
            RT[m + 32, m] = -1.0
        else:
            RT[m - 32, m] = 1.0
    NEG = -30000.0
    kj = np.arange(128)[:, None]
    qi = np.arange(128)[None, :]
    tri_prev = np.where(kj >= qi, 0.0, NEG).astype(np.float32)
    tri_next = np.where(kj <= qi, 0.0, NEG).astype(np.float32)
    allm = np.full((128, 128), NEG, np.float32)
    m_first = allm if half == 0 else tri_prev
    m_last = tri_next if half == 0 else allm
    ms = np.stack([m for m in (tri_prev, tri_next, m_first, m_last)], axis=1)
    return RT, np.ascontiguousarray(ms.astype(np.float32))


_NC_CACHE = {}


def prep_inputs(x, c, ctx, c_ctx, w_ada, b_ada, norm_g, w_ffn_in, w_ffn_out, w_in,
                gmlp_ln_g, gmlp_ln_b, w_spatial, b_spatial, attn_sink,
                w_branch_a, w_branch_b, w_out, cores=range(8)):
    f = lambda a: np.ascontiguousarray(np.asarray(a, dtype=np.float32))
    x, c, ctx, c_ctx = f(x), f(c), f(ctx), f(c_ctx)
    shared = {
        "w_ada": f(w_ada)[0],
        "bpp": np.ascontiguousarray(f(b_ada)[0].reshape(144, 128).T),
        "gpp": np.ascontiguousarray(f(norm_g)[0].reshape(96, 128).T),
        "w_ffn_in": f(w_ffn_in)[0], "w_ffn_out": f(w_ffn_out)[0], "w_in": f(w_in)[0],
        "lngb": np.ascontiguousarray(np.concatenate([f(gmlp_ln_g)[0].reshape(8, 128).T, f(gmlp_ln_b)[0].reshape(8, 128).T], axis=1)),
        "w_sT": np.ascontiguousarray(f(w_spatial)[0].transpose(2, 0, 1)),
        "b_sp": f(b_spatial)[0].reshape(1, 1024),
        "sink": f(attn_sink)[0].reshape(1, 8),
        "w_bra": f(w_branch_a)[0], "w_brb": f(w_branch_b)[0], "w_o": f(w_out)[0],
    }
    in_maps = []
    for core in cores:
        b, half = core // 2, core % 2
        own = np.arange(half * 2048, (half + 1) * 2048)
        halo = np.arange(2048, 2176) if half == 0 else np.arange(1920, 2048)
        xe = np.concatenate([x[b, own], x[b, halo], ctx[b]], axis=0)
        cosT, sinT = _rope_tables(np.concatenate([own, halo]))
        RT, ms = _const_tables(half)
        cvec = np.concatenate([c[b].reshape(16, 128).T, c_ctx.reshape(16, 128).T], axis=1)
        m = dict(shared)
        m.update({"xe": np.ascontiguousarray(xe), "cvec": np.ascontiguousarray(cvec), "cosT": cosT, "sinT": sinT,
                  "rmatT": RT, "masks": ms})
        in_maps.append(m)
    return in_maps


def kernel(**inputs):
    in_maps = prep_inputs(**inputs)
    if "nc" not in _NC_CACHE:
        _NC_CACHE["nc"] = build_program()
    res = run_bass_kernel_spmd(_NC_CACHE["nc"], in_maps, core_ids=list(range(8)))
    out = np.empty((4, 4096, 2048), np.float32)
    for core in range(8):
        b, half = core // 2, core % 2
        out[b, half * 2048:(half + 1) * 2048] = res.results[core]["out"]
    return out
```
